# Optimizing a Trainium2 kernel written in Bass

```python
import jax, jax.numpy as jnp
from jax import lax
import numpy as np

D_MODEL = 1024
BATCH = 2
SEQ = 8192
DEPTH = 1

CHUNK = 64
Q_BLOCK = 128
ATTN_HEADS = 8
HEAD_DIM = 64
ATTN_WIDTH = ATTN_HEADS * HEAD_DIM
CONV_WIDTH = D_MODEL - ATTN_WIDTH
CONV_GROUPS = 8
CONV_KERNEL = 31
IDX_HEADS = 8
IDX_DIM = 64
TOPK_MAX = 256
ROT_DIM = HEAD_DIM // 4
ROPE_THETA = 500000.0
D_FF = 4 * D_MODEL
N_MOD = 6
EPS = 1e-6
IN_SIZES = (ATTN_WIDTH, ATTN_WIDTH, ATTN_WIDTH, IDX_HEADS * IDX_DIM, IDX_DIM, IDX_HEADS, CONV_WIDTH, CONV_WIDTH)
IN_COLS = 3 * ATTN_WIDTH + IDX_HEADS * IDX_DIM + IDX_DIM + IDX_HEADS + 2 * CONV_WIDTH

kernel_name = "hybrid_dsa_conformer_adaln_layer"


def _split_points():
    pts, acc = [], 0
    for s in IN_SIZES[:-1]:
        acc += s
        pts.append(acc)
    return pts


def rms_norm(x, g):
    xf = x.astype(jnp.float32)
    y = xf * lax.rsqrt(jnp.mean(xf * xf, axis=-1, keepdims=True) + EPS)
    return (y * g.astype(jnp.float32)).astype(x.dtype)


def layer_norm(x, g, b):
    xf = x.astype(jnp.float32)
    mu = jnp.mean(xf, axis=-1, keepdims=True)
    var = jnp.mean(jnp.square(xf - mu), axis=-1, keepdims=True)
    y = (xf - mu) * lax.rsqrt(var + EPS)
    return (y * g.astype(jnp.float32) + b.astype(jnp.float32)).astype(x.dtype)


def rope_tables(seq_len, dtype):
    pos = jnp.arange(seq_len, dtype=jnp.float32)
    inv_freq = ROPE_THETA ** (-jnp.arange(0, ROT_DIM, 2, dtype=jnp.float32) / ROT_DIM)
    ang = pos[:, None] * inv_freq[None, :]
    return jnp.cos(ang).astype(dtype), jnp.sin(ang).astype(dtype)


def partial_rope(x, cos, sin):
    half = ROT_DIM // 2
    x1 = x[..., :half]
    x2 = x[..., half:ROT_DIM]
    c = cos[None, :, None, :]
    s = sin[None, :, None, :]
    return jnp.concatenate([x1 * c - x2 * s, x2 * c + x1 * s, x[..., ROT_DIM:]], axis=-1)


def dsa_sparse_attention(q, k, v, iq, ik, iw):
    B, S, H, dh = q.shape
    n_blk = S // Q_BLOCK
    top_k = min(TOPK_MAX, S // 4)
    key_chunk = jnp.arange(S) // CHUNK
    ik_f = ik.astype(jnp.float32)

    def block(i):
        start = i * Q_BLOCK
        q_b = lax.dynamic_slice_in_dim(q, start, Q_BLOCK, axis=1)
        iq_b = lax.dynamic_slice_in_dim(iq, start, Q_BLOCK, axis=1).astype(jnp.float32)
        iw_b = lax.dynamic_slice_in_dim(iw, start, Q_BLOCK, axis=1).astype(jnp.float32)
        q_chunk = (start + jnp.arange(Q_BLOCK)) // CHUNK
        admissible = key_chunk[None, :] <= q_chunk[:, None]
        rel = jax.nn.relu(jnp.einsum('bthd,bsd->bths', iq_b, ik_f) * (IDX_DIM ** -0.5))
        score = jnp.einsum('bths,bth->bts', rel, iw_b)
        score = jnp.where(admissible[None], score, -jnp.inf)
        _, idx = lax.top_k(score, top_k)
        valid = key_chunk[idx] <= q_chunk[None, :, None]
        k_sel = jax.vmap(lambda kb, ib: kb[ib])(k, idx)
        v_sel = jax.vmap(lambda vb, ib: vb[ib])(v, idx)
        logits = jnp.einsum('bthd,btkhd->bthk', q_b, k_sel).astype(jnp.float32) * (dh ** -0.5)
        logits = jnp.where(valid[:, :, None, :], logits, -jnp.inf)
        p = jax.nn.softmax(logits, axis=-1).astype(v.dtype)
        return jnp.einsum('bthk,btkhd->bthd', p, v_sel)

    out = lax.map(block, jnp.arange(n_blk))
    return jnp.transpose(out, (1, 0, 2, 3, 4)).reshape(B, S, H * dh)


def conformer_conv(a, g, w_dw, b_dw, ln_g, ln_b):
    u = a * jax.nn.sigmoid(g)
    u = jnp.pad(u, ((0, 0), (CONV_KERNEL - 1, 0), (0, 0)))
    y = lax.conv_general_dilated(u, w_dw[:, None, :], window_strides=(1,), padding='VALID',
                                 dimension_numbers=('NWC', 'WIO', 'NWC'),
                                 feature_group_count=CONV_WIDTH) + b_dw
    y = layer_norm(y, ln_g, ln_b)
    return jax.nn.silu(y)


def setup_inputs(seed: int = 0) -> dict:
    key = jax.random.key(seed)
    ks = jax.random.split(key, 20)

    def nrm(k, shape, scale):
        return jax.random.normal(k, shape, jnp.float32) * scale

    L = DEPTH
    return {
        "x": nrm(ks[0], (BATCH, SEQ, D_MODEL), 1.0),
        "c": nrm(ks[1], (BATCH, D_MODEL), 1.0),
        "w_ada": nrm(ks[2], (L, D_MODEL, N_MOD * D_MODEL), 0.5 * D_MODEL ** -0.5),
        "b_ada": nrm(ks[3], (L, N_MOD * D_MODEL), 0.02),
        "g_norm1": 1.0 + nrm(ks[4], (L, D_MODEL), 0.02),
        "w_in": nrm(ks[5], (L, D_MODEL, IN_COLS), D_MODEL ** -0.5),
        "g_q": 1.0 + nrm(ks[6], (L, HEAD_DIM), 0.02),
        "g_k": 1.0 + nrm(ks[7], (L, HEAD_DIM), 0.02),
        "w_dw": nrm(ks[8], (L, CONV_KERNEL, CONV_WIDTH), CONV_KERNEL ** -0.5),
        "b_dw": nrm(ks[9], (L, CONV_WIDTH), 0.02),
        "g_conv_ln": 1.0 + nrm(ks[10], (L, CONV_WIDTH), 0.02),
        "b_conv_ln": nrm(ks[11], (L, CONV_WIDTH), 0.02),
        "g_out_attn": 1.0 + nrm(ks[12], (L, ATTN_WIDTH), 0.02),
        "g_out_conv": 1.0 + nrm(ks[13], (L, CONV_WIDTH), 0.02),
        "w_out": nrm(ks[14], (L, D_MODEL, D_MODEL), D_MODEL ** -0.5),
        "g_norm2": 1.0 + nrm(ks[15], (L, D_MODEL), 0.02),
        "w_ff1": nrm(ks[16], (L, D_MODEL, D_FF), D_MODEL ** -0.5),
        "w_ff2": nrm(ks[17], (L, D_FF, D_MODEL), D_FF ** -0.5),
    }


def reference(x, c, w_ada, b_ada, g_norm1, w_in, g_q, g_k, w_dw, b_dw, g_conv_ln, b_conv_ln,
              g_out_attn, g_out_conv, w_out, g_norm2, w_ff1, w_ff2):
    B, S, _ = x.shape
    cos, sin = rope_tables(S, x.dtype)
    c_act = jax.nn.silu(c)
    split_pts = _split_points()
    for l in range(DEPTH):
        mod = (c_act @ w_ada[l] + b_ada[l])[:, None, :]
        sh1, sc1, gt1, sh2, sc2, gt2 = jnp.split(mod, N_MOD, axis=-1)

        h = rms_norm(x, g_norm1[l]) * (1.0 + sc1) + sh1
        u = h @ w_in[l]
        q, k, v, iq, ik, iw, ca, cg = jnp.split(u, split_pts, axis=-1)
        q = partial_rope(rms_norm(q.reshape(B, S, ATTN_HEADS, HEAD_DIM), g_q[l]), cos, sin)
        k = partial_rope(rms_norm(k.reshape(B, S, ATTN_HEADS, HEAD_DIM), g_k[l]), cos, sin)
        v = v.reshape(B, S, ATTN_HEADS, HEAD_DIM)
        iq = partial_rope(iq.reshape(B, S, IDX_HEADS, IDX_DIM), cos, sin)
        ik = partial_rope(ik.reshape(B, S, 1, IDX_DIM), cos, sin)[:, :, 0, :]
        iw = iw * (IDX_HEADS ** -0.5)
        attn = dsa_sparse_attention(q, k, v, iq, ik, iw)
        conv = conformer_conv(ca, cg, w_dw[l], b_dw[l], g_conv_ln[l], b_conv_ln[l])
        mixed = jnp.concatenate([rms_norm(attn, g_out_attn[l]), rms_norm(conv, g_out_conv[l])], axis=-1)
        x = x + gt1 * (mixed @ w_out[l])

        h2 = rms_norm(x, g_norm2[l]) * (1.0 + sc2) + sh2
        f = jnp.square(jax.nn.relu(h2 @ w_ff1[l])) @ w_ff2[l]
        x = x + gt2 * f
    return x
```

```python
import numpy as np
from contextlib import ExitStack
import concourse.bass as bass
import concourse.mybir as mybir
from concourse.bass_utils import run_bass_kernel_spmd

F32 = mybir.dt.float32
BF16 = mybir.dt.bfloat16
U8 = mybir.dt.uint8
AF = mybir.ActivationFunctionType
OP = mybir.AluOpType
AX = mybir.AxisListType

S_LEN = 8192
D = 1024
NT = 64
NG = 16
NSLOT = 16
CQ, CK, CV, CIQ, CIK, CIW, CCA, CCG = 0, 512, 1024, 1536, 2048, 2112, 2120, 2632
NCOL = 3144
EPS = 1e-6
NBIS = 14
BIS_R = 8.0
NEG = -1.0e30


_SLOT_NG = [1, 3, 5, 7, 9, 11, 13, 15, 16, 14, 12, 10, 8, 6, 4, 2]


def _slot_pe(n):
    ng = _SLOT_NG[n]
    return (ng - 1, 0) if ng <= 8 else (16 - ng, 1)


def slot_tile(j, n):
    p, e = _slot_pe(n)
    return 4 * p + j if e == 0 else 63 - 4 * p - j


def slot_ngroups(n):
    p, e = _slot_pe(n)
    return p + 1 if e == 0 else 16 - p


class Buf:
    def __init__(self, name):
        self.name = name
        self.lw = None
        self.rd = []


class Sched:
    def __init__(self, nc, es, n_dma_sems=24):
        self.nc = nc
        self.eng = {"pe": nc.tensor, "act": nc.scalar, "dve": nc.vector, "pool": nc.gpsimd, "sp": nc.sync}
        self.sem = {}
        self.cnt = {}
        for e in ["pe", "act", "dve", "pool"]:
            self.sem[e] = es.enter_context(nc.semaphore("c_" + e))
            self.cnt[e] = 0
        self.dsems = []
        for i in range(n_dma_sems):
            k = "d%d" % i
            self.sem[k] = es.enter_context(nc.semaphore(k))
            self.cnt[k] = 0
            self.dsems.append(k)
        self.dnext = 0
        self.rec = None
        self.seen = {e: {} for e in self.eng}

    def _wait(self, e, deps):
        best = {}
        for d in deps:
            if d is None:
                continue
            k, v = d
            if v <= 0 or (k == e and e == "pe"):
                continue
            if best.get(k, 0) < v:
                best[k] = v
        for k, v in best.items():
            if self.seen[e].get(k, 0) < v:
                self.eng[e].wait_ge(self.sem[k], v)
                self.seen[e][k] = v

    def _deps(self, e, reads, writes):
        deps = []
        for b in reads:
            deps.append(b.lw)
        for b in writes:
            deps.append(b.lw)
            for r in b.rd:
                deps.append(r)
        return deps

    def _mark(self, tok, reads, writes):
        for b in writes:
            b.lw = tok
            b.rd = []
        for b in reads:
            b.rd.append(tok)

    def op(self, e, fn, reads=(), writes=()):
        if self.rec is not None:
            self.rec.append(("op", e, fn, tuple(reads), tuple(writes)))
            return None
        self._wait(e, self._deps(e, reads, writes))
        inst = fn(self.eng[e])
        self.cnt[e] += 1
        inst.then_inc(self.sem[e], 1)
        tok = (e, self.cnt[e])
        self._mark(tok, reads, writes)
        return tok

    def dma(self, out, in_, reads=(), writes=(), q="sp", **kw):
        if self.rec is not None:
            self.rec.append(("dma", out, in_, tuple(reads), tuple(writes), q, kw))
            return None
        k = self.dsems[self.dnext]
        self.dnext = (self.dnext + 1) % len(self.dsems)
        deps = self._deps(q, reads, writes)
        deps.append((k, self.cnt[k]))
        self._wait(q, deps)
        self.eng[q].dma_start(out=out, in_=in_, **kw).then_inc(self.sem[k], 16)
        self.cnt[k] += 16
        tok = (k, self.cnt[k])
        self._mark(tok, reads, writes)
        return tok

    def mark(self):
        self.rec.append(("mark",))

    def record(self, fn):
        self.rec = []
        fn()
        r, self.rec = self.rec, None
        segs = [[]]
        for it in r:
            if it[0] == "mark":
                segs.append([])
            else:
                segs[-1].append(it)
        return segs

    def emit(self, it):
        if it[0] == "op":
            self.op(it[1], it[2], it[3], it[4])
        else:
            self.dma(it[1], it[2], it[3], it[4], it[5], **it[6])

    def pipeline(self, tiles, lead=None, hold_drain=False):
        T = len(tiles)
        nst = max(len(t) for t in tiles)
        steps = []
        for s_ in range(T + nst - 1):
            steps.append([tiles[s_ - k][k] for k in reversed(range(nst)) if 0 <= s_ - k < T and k < len(tiles[s_ - k])])
        if lead:
            for i_, l_ in enumerate(lead):
                if i_ < len(steps):
                    steps[i_] = list(l_) + steps[i_]
                else:
                    steps.append(list(l_))
        n_emit = T if hold_drain else len(steps)
        for act in steps[:n_emit]:
            pos = [0] * len(act)
            more = True
            while more:
                more = False
                for a, seg in enumerate(act):
                    if pos[a] < len(seg):
                        self.emit(seg[pos[a]])
                        pos[a] += 1
                        more = True
        return steps[n_emit:]

    def soft_barrier(self, bufs):
        toks = []
        for b in bufs:
            toks.append(b.lw)
            toks.extend(b.rd)
        for e in self.eng:
            self._wait(e, toks)

    def barrier(self):
        toks = [(k, v) for k, v in self.cnt.items()]
        for e in self.eng:
            self._wait(e, toks)


def build_nc(dbg=None):
    nc = bass.Bass("TRN2", target_bir_lowering=False)
    di = lambda name, shape, dt=F32: nc.dram_tensor(name, shape, dt, kind="ExternalInput").ap()
    xb = di("xb", [S_LEN, D])
    xo = di("xo", [NSLOT * 128, D])
    xhal = di("xhal", [NSLOT * 32, D])
    ccol = di("ccol", [128, 8])
    w_ada = di("w_ada", [D, 6 * D])
    bada_col = di("bada_col", [128, 48])
    bada_row = di("bada_row", [6 * D])
    g1col = di("g1col", [128, 8])
    g2col = di("g2col", [128, 8])
    w_in = di("w_in", [D, NCOL])
    gq = di("gq", [64])
    gk = di("gk", [64])
    wdwT = di("wdwT", [128, 4, 31])
    bdw_col = di("bdw_col", [128, 4])
    gln = di("gln", [512])
    bln = di("bln", [512])
    gcat_col = di("gcat_col", [128, 8])
    w_out = di("w_out", [D, D])
    w_ff1 = di("w_ff1", [D, 4 * D])
    w_ff2 = di("w_ff2", [4 * D, D])
    cs_all = di("cs_all", [128, NT, 16])
    cs_own = di("cs_own", [128, NSLOT, 16])
    limrel = di("limrel", [128, NSLOT])
    tauovr = di("tauovr", [128, NSLOT])
    hflag = di("hflag", [128, NSLOT])
    y = nc.dram_tensor("y", [NSLOT * 128, D], F32, kind="ExternalOutput").ap()

    scr = lambda name, shape, dt=BF16: nc.dram_tensor(name, shape, dt, kind="Internal").ap()
    KT_scr = scr("KT_scr", [NG, 128, 4, 512])
    V_scr = scr("V_scr", [NG, 128, 4, 520])
    qT_scr = scr("qT_scr", [NSLOT, 128, 4, 128])
    iqT_scr = scr("iqT_scr", [NSLOT, 128, 4, 128])
    mix_scr = scr("mix_scr", [NSLOT, 128, 8, 128])
    b_KTs = [Buf("KTs%d" % g) for g in range(NG)]
    b_Vs = [Buf("Vs%d" % g) for g in range(NG)]
    b_qTs = [Buf("qTs%d" % n) for n in range(NSLOT)]
    b_iqTs = [Buf("iqTs%d" % n) for n in range(NSLOT)]
    b_mixa = [Buf("mixa%d" % n) for n in range(NSLOT)]
    b_mixc = [Buf("mixc%d" % n) for n in range(NSLOT)]
    b_y = [Buf("y%d" % n) for n in range(NSLOT)]

    dbg_out = {}
    if dbg:
        for name, (shape, dt) in dbg.items():
            dbg_out[name] = nc.dram_tensor("dbg_" + name, shape, dt, kind="ExternalOutput").ap()
    b_dbg = Buf("dbg")

    with ExitStack() as es:
        S = Sched(nc, es)

        def sbt(st, name, shape, dt):
            return st.enter_context(nc.sbuf_tensor(name, shape, dt)), Buf(name)

        def dump(name, ap, buf):
            if name in dbg_out:
                S.dma(dbg_out[name], ap, reads=[buf], writes=[b_dbg])

        PB = []
        bPB = []
        for i in range(8):
            PB.append(es.enter_context(nc.psum_tensor("pb%d" % i, [128, 512], F32)))
            bPB.append(Buf("pb%d" % i))
        PBh = [p.bitcast(BF16) for p in PB]

        ident_f, b_identf = sbt(es, "ident_f", [128, 128], F32)
        ident_b, b_identb = sbt(es, "ident_b", [128, 128], BF16)
        modcol, b_modcol = sbt(es, "modcol", [128, 4, 8], F32)
        gt_row, b_gtrow = sbt(es, "gt_row", [128, 2, 1024], F32)
        gmod, b_gmod = sbt(es, "gmod", [128, 2, 8], F32)
        gcat, b_gcat = sbt(es, "gcat", [128, 8], F32)
        small, b_small = sbt(es, "small", [128, 64], F32)

        S.op("pool", lambda g: g.memset(ident_f[:], 0.0), writes=[b_identf])
        S.op("pool", lambda g: g.affine_select(out=ident_f[:], in_=ident_f[:], pattern=[[-1, 128]],
                                               compare_op=OP.not_equal, fill=1.0, base=0, channel_multiplier=1),
             reads=[b_identf], writes=[b_identf])
        S.op("dve", lambda v: v.tensor_copy(ident_b[:], ident_f[:]), reads=[b_identf], writes=[b_identb])
        S.dma(gcat[:], gcat_col, writes=[b_gcat])

        mhalf, b_mhalf = sbt(es, "mhalf", [128, 8], F32)
        S.op("pool", lambda g: g.memset(mhalf[:], -0.5), writes=[b_mhalf])

        def rstd_from_ssq(dst, b_dst, n_part, inv_n, nfree=1):
            w = dst.shape[-1]
            S.op("dve", lambda v: v.tensor_scalar(out=dst, in0=dst, scalar1=inv_n, scalar2=EPS,
                                                  op0=OP.mult, op1=OP.add), reads=[b_dst], writes=[b_dst])
            S.op("pool", lambda g: g.tensor_tensor(out=dst, in0=dst, in1=mhalf[0:n_part, 0:w], op=OP.pow),
                 reads=[b_dst, b_mhalf], writes=[b_dst])

        with ExitStack() as ea:
            IKT2, b_IKT2 = sbt(ea, "IKT2", [128, S_LEN], BF16)
            iw_all, b_iw = sbt(ea, "iw_all", [128, NSLOT, 8], F32)
            lim_sb, b_lim = sbt(ea, "lim_sb", [128, NSLOT], F32)
            hfl_sb, b_hfl = sbt(ea, "hfl_sb", [128, NSLOT], F32)
            ovr_sb, b_ovr = sbt(ea, "ovr_sb", [128, NSLOT], F32)
            p012 = ExitStack()
            Wp, b_Wp = sbt(p012, "Wp", [128, 8, NCOL], BF16)
            bias_row, b_biasrow = sbt(p012, "bias_row", [128, CCA], F32)
            bias_cc, b_biascc = sbt(p012, "bias_cc", [128, 8], F32)
            csA, b_csA = sbt(p012, "csA", [128, NT, 16], F32)
            csO, b_csO = sbt(p012, "csO", [128, NSLOT, 16], F32)
            gq_row, b_gq = sbt(p012, "gq_row", [128, 64], F32)
            gk_row, b_gk = sbt(p012, "gk_row", [128, 64], F32)
            S.dma(csA[:], cs_all, writes=[b_csA])
            S.dma(csO[:], cs_own, writes=[b_csO])
            S.dma(gq_row[:], gq.partition_broadcast(128), writes=[b_gq])
            S.dma(gk_row[:], gk.partition_broadcast(128), writes=[b_gk])
            S.dma(lim_sb[:], limrel, writes=[b_lim])
            S.dma(hfl_sb[:], hflag, writes=[b_hfl])
            S.dma(ovr_sb[:], tauovr, writes=[b_ovr])

            silu_c, b_silu = sbt(p012, "silu_c", [128, 8], F32)
            badac, b_badac = sbt(p012, "badac", [128, 48], F32)
            screp_box = {}
            g12, b_g12 = sbt(p012, "g12", [128, 2, 8], F32)
            colidx = {0: 0, 1: 1, 3: 2, 4: 3}

            c2, b_c2 = sbt(p012, "c2", [128, 8, 2], BF16)
            mtmp, b_mtmp = sbt(p012, "mtmp", [128, 8], F32)

            def mod_block(m, wa, b_wa, wab, b_wab, pcol, prow):
                S.dma(wa[:], w_ada[:, m * D:(m + 1) * D].rearrange("(kc p) n -> p kc n", p=128), writes=[b_wa])
                S.op("act", lambda a: a.activation(out=wab[:], in_=wa[:], func=AF.Copy), reads=[b_wa], writes=[b_wab])
                if m in colidx:
                    def mm(t):
                        last = None
                        for fc in range(8):
                            for kc in range(8):
                                last = t.matmul(PB[pcol][:, 2 * fc:2 * fc + 2], lhsT=wab[:, kc, fc * 128:(fc + 1) * 128],
                                                rhs=c2[:, kc, :], start=(kc == 0), stop=(kc == 7))
                        return last
                    S.op("pe", mm, reads=[b_wab, b_c2], writes=[bPB[pcol]])
                    S.op("dve", lambda v: v.tensor_reduce(out=mtmp[:, :], in_=PB[pcol][:, 0:16].rearrange("p (f e) -> p f e", e=2),
                                                          axis=AX.X, op=OP.add), reads=[bPB[pcol]], writes=[b_mtmp])
                    S.op("dve", lambda v: v.tensor_tensor(out=modcol[:, colidx[m], :], in0=mtmp[:, :],
                                                          in1=badac[:, m * 8:(m + 1) * 8], op=OP.add),
                         reads=[b_mtmp, b_badac], writes=[b_modcol])
                    if m in (1, 4):
                        i = 0 if m == 1 else 1
                        S.op("dve", lambda v: v.scalar_tensor_tensor(out=gmod[:, i, :], in0=modcol[:, colidx[m], :], scalar=1.0,
                                                                     in1=g12[:, i, :], op0=OP.add, op1=OP.mult),
                             reads=[b_modcol, b_g12], writes=[b_gmod])
                else:
                    gi = 0 if m == 2 else 1
                    for cg_ in range(2):
                        def half(cg):
                            pb_ = prow[cg]

                            sc_rep2, b_screp, badar, b_badar = screp_box["v"]

                            def mm(t):
                                last = None
                                for kc in range(8):
                                    for e in range(2):
                                        last = t.matmul(PB[pb_][:, :], lhsT=sc_rep2[:, kc, e, :], rhs=wab[:, kc, cg * 512:(cg + 1) * 512],
                                                        start=(kc == 0 and e == 0), stop=(kc == 7 and e == 1))
                                return last
                            S.op("pe", mm, reads=[b_wab, b_screp], writes=[bPB[pb_]])
                            S.op("dve", lambda v: v.tensor_tensor(out=gt_row[:, gi, cg * 512:(cg + 1) * 512], in0=PB[pb_][:, :],
                                                                  in1=badar[:, gi, cg * 512:(cg + 1) * 512], op=OP.add),
                                 reads=[bPB[pb_], b_badar], writes=[b_gtrow])
                        half(cg_)

            with ExitStack() as p0:
                c_sb, b_c = sbt(p0, "c_sb", [128, 8], F32)
                sh_rep, b_shrep = sbt(p0, "sh_rep", [128, 8, 128], F32)
                was = [sbt(p0, "wa%d" % i, [128, 8, 1024], F32) for i in range(2)]
                wab0, b_wab0 = sbt(p0, "wab0", [128, 8, 1024], BF16)
                stw = [sbt(p0, "stw%d" % i, [128, NCOL], F32) for i in range(2)]
                S.dma(c_sb[:], ccol, writes=[b_c])
                S.dma(badac[:], bada_col, writes=[b_badac])
                S.dma(g12[:, 0, :], g1col, writes=[b_g12])
                S.dma(g12[:, 1, :], g2col, writes=[b_g12])
                S.op("act", lambda a: a.activation(out=silu_c[:], in_=c_sb[:], func=AF.Silu),
                     reads=[b_c], writes=[b_silu])
                S.op("dve", lambda v: v.tensor_copy(c2[:, :, 0], silu_c[:, :]), reads=[b_silu], writes=[b_c2])
                S.op("dve", lambda v: v.tensor_tensor(out=c2[:, :, 1], in0=silu_c[:, :], in1=c2[:, :, 0], op=OP.subtract),
                     reads=[b_silu, b_c2], writes=[b_c2])
                for m in range(2):
                    mod_block(m, was[m][0], was[m][1], wab0, b_wab0, 5, (6, 7))
                for kc in range(8):
                    S.op("dve", lambda v: v.tensor_copy(sh_rep[:, kc, :], modcol[:, 0, kc:kc + 1].to_broadcast([128, 128])),
                         reads=[b_modcol], writes=[b_shrep])
                S.op("pool", lambda g: g.memset(bias_cc[:], 0.0), writes=[b_biascc])
                rgroups = [(0, 512), (512, 512), (1024, 512), (1536, 512), (2048, 72)]
                for kc in range(8):
                    st, b_st = stw[kc % 2]
                    S.dma(st[:], w_in[kc * 128:(kc + 1) * 128, :], writes=[b_st])
                    for gi, (c0, cn) in enumerate(rgroups):
                        S.op("pe", lambda t: t.matmul(PB[gi][:, 0:cn], lhsT=sh_rep[:, kc, :], rhs=st[:, c0:c0 + cn],
                                                      start=(kc == 0), stop=(kc == 7)),
                             reads=[b_st, b_shrep], writes=[bPB[gi]])

                    def mmc(t):
                        last = None
                        for j in range(8):
                            last = t.matmul(PB[5][:, j:j + 1], lhsT=st[:, CCA + j * 128:CCA + (j + 1) * 128],
                                            rhs=modcol[:, 0, kc:kc + 1], start=True, stop=True)
                        return last
                    S.op("pe", mmc, reads=[b_st, b_modcol], writes=[bPB[5]])
                    S.op("dve", lambda v: v.tensor_tensor(out=bias_cc[:], in0=PB[5][:, 0:8], in1=bias_cc[:], op=OP.add),
                         reads=[bPB[5], b_biascc], writes=[b_biascc])
                    S.op("act", lambda a: a.activation(out=Wp[:, kc, :], in_=st[:], func=AF.Copy, scale=gmod[:, 0, kc:kc + 1]),
                         reads=[b_st, b_gmod], writes=[b_Wp])
                for gi, (c0, cn) in enumerate(rgroups):
                    S.op("act", lambda a: a.activation(out=bias_row[:, c0:c0 + cn], in_=PB[gi][:, 0:cn], func=AF.Copy),
                         reads=[bPB[gi]], writes=[b_biasrow])
                dump("modcol", modcol[:], b_modcol)
                dump("gt_row", gt_row[:], b_gtrow)
                dump("bias_row", bias_row[:], b_biasrow)
                dump("bias_cc", bias_cc[:], b_biascc)
                S.barrier()

            def norm_transpose(x_ap, b_x, npart, xh_t, b_xh, junk_t, b_junk, ssq_ap, b_ssq, pbank, xT_ap, b_xT):
                S.op("act", lambda a: a.activation(out=xh_t[:npart, :], in_=x_ap, func=AF.Square, accum_out=ssq_ap),
                     reads=[b_x], writes=[b_xh, b_ssq])
                rstd_from_ssq(ssq_ap, b_ssq, npart, 1.0 / D)
                S.op("act", lambda a: a.activation(out=xh_t[:npart, :], in_=x_ap, func=AF.Copy, scale=ssq_ap),
                     reads=[b_x, b_ssq], writes=[b_xh])

                def tr(t):
                    last = None
                    for kc in range(8):
                        last = t.transpose(PBh[pbank][:, kc * npart:(kc + 1) * npart],
                                           xh_t[:npart, kc * 128:(kc + 1) * 128], ident_b[:npart, :npart])
                    return last
                S.op("pe", tr, reads=[b_xh, b_identb], writes=[bPB[pbank]])
                S.op("dve", lambda v: v.tensor_copy(xT_ap, PBh[pbank][:, 0:8 * npart].rearrange("p (k t) -> p k t", k=8)),
                     reads=[bPB[pbank]], writes=[b_xT])

            def headnorm_rope(src_f, b_src, g_row, b_g, cs_ap, b_cs, dst_b, b_dst, tmp, b_tmp, st8, b_st8, nheads, do_norm,
                              tmpr=None, b_tmpr=None, mark_mid=False):
                sv = src_f.rearrange("p (h d) -> p h d", h=nheads)
                dv = dst_b.rearrange("p (h d) -> p h d", h=nheads)
                if do_norm:
                    tv = tmp[:, 0:nheads * 64]
                    S.op("act", lambda a: a.activation(out=tv, in_=src_f, func=AF.Square), reads=[b_src], writes=[b_tmp])
                    S.op("dve", lambda v: v.tensor_reduce(out=st8[:, 0:nheads], in_=tv.rearrange("p (h d) -> p h d", h=nheads),
                                                          axis=AX.X, op=OP.add), reads=[b_tmp], writes=[b_st8])
                    rstd_from_ssq(st8[:, 0:nheads], b_st8, 128, 1.0 / 64)
                    S.op("dve", lambda g: g.tensor_tensor(out=sv, in0=sv,
                                                          in1=st8[:, 0:nheads].unsqueeze(2).to_broadcast([128, nheads, 64]),
                                                          op=OP.mult), reads=[b_src, b_st8], writes=[b_src])
                    S.op("dve", lambda g: g.tensor_tensor(out=sv, in0=sv,
                                                          in1=g_row[:, :].unsqueeze(1).to_broadcast([128, nheads, 64]),
                                                          op=OP.mult), reads=[b_src, b_g], writes=[b_src])
                if mark_mid:
                    S.mark()
                S.op("act", lambda a: a.activation(out=dst_b, in_=src_f, func=AF.Copy), reads=[b_src], writes=[b_dst])
                cosb = cs_ap[:, 0:8].unsqueeze(1).to_broadcast([128, nheads, 8])
                sinb = cs_ap[:, 8:16].unsqueeze(1).to_broadcast([128, nheads, 8])
                if tmpr is None:
                    t1 = tmp[:, 512:512 + nheads * 8].rearrange("p (h d) -> p h d", h=nheads)
                    t2 = tmp[:, 576:576 + nheads * 8].rearrange("p (h d) -> p h d", h=nheads)
                else:
                    t1 = tmpr[:, 0:nheads * 8].rearrange("p (h d) -> p h d", h=nheads)
                    t2 = tmpr[:, 64:64 + nheads * 8].rearrange("p (h d) -> p h d", h=nheads)
                    b_tmp = b_tmpr
                x1 = sv[:, :, 0:8]
                x2 = sv[:, :, 8:16]
                S.op("pool", lambda g: g.tensor_tensor(out=t1, in0=x1, in1=cosb, op=OP.mult), reads=[b_src, b_cs], writes=[b_tmp])
                S.op("pool", lambda g: g.tensor_tensor(out=t2, in0=x2, in1=sinb, op=OP.mult), reads=[b_src, b_cs], writes=[b_tmp])
                S.op("pool", lambda g: g.tensor_tensor(out=dv[:, :, 0:8], in0=t1, in1=t2, op=OP.subtract),
                     reads=[b_tmp], writes=[b_dst])
                S.op("pool", lambda g: g.tensor_tensor(out=t1, in0=x2, in1=cosb, op=OP.mult), reads=[b_src, b_cs], writes=[b_tmp])
                S.op("pool", lambda g: g.tensor_tensor(out=t2, in0=x1, in1=sinb, op=OP.mult), reads=[b_src, b_cs], writes=[b_tmp])
                S.op("pool", lambda g: g.tensor_tensor(out=dv[:, :, 8:16], in0=t1, in1=t2, op=OP.add),
                     reads=[b_tmp], writes=[b_dst])

            def proj_tok(pbank, xT_ap, b_xT, c0, cn):
                def mm(t):
                    last = None
                    for kc in range(8):
                        last = t.matmul(PB[pbank][:, 0:cn], lhsT=xT_ap[:, kc, :], rhs=Wp[:, kc, c0:c0 + cn],
                                        start=(kc == 0), stop=(kc == 7))
                    return last
                S.op("pe", mm, reads=[b_xT, b_Wp], writes=[bPB[pbank]])

            with ExitStack() as p12:
                xt = [sbt(p12, "xt%d" % i, [128, D], F32) for i in range(2)]
                xh, b_xh = sbt(p12, "xh", [128, D], BF16)
                junk, b_junk = xh, b_xh
                ssq, b_ssq = sbt(p12, "ssq", [128, 2], F32)
                xT = [sbt(p12, "xT%d" % i, [128, 8, 160], BF16) for i in range(4)]
                kf = [sbt(p12, "kf%d" % i, [128, 512], F32) for i in range(3)]
                ikf = [sbt(p12, "ikf%d" % i, [128, 64], F32) for i in range(3)]
                tmp, b_tmp = sbt(p12, "tmp", [128, 640], F32)
                tmpB, b_tmpB = sbt(p12, "tmpB", [128, 128], F32)
                tmpC, b_tmpC = sbt(p12, "tmpC", [128, 128], F32)
                st8, b_st8 = sbt(p12, "st8", [128, 8], F32)
                kb = [sbt(p12, "kb%d" % i, [128, 512], BF16) for i in range(2)]
                ikb = [sbt(p12, "ikb%d" % i, [128, 128], BF16) for i in range(2)]
                KTg = [sbt(p12, "KTg%d" % i, [128, 4, 512], BF16) for i in range(2)]
                Vg = [sbt(p12, "Vg%d" % i, [128, 4, 520], BF16) for i in range(2)]
                for i in range(2):
                    S.op("pool", lambda g: g.memset(Vg[i][0][:], 1.0), writes=[Vg[i][1]])

                def p1_tile(i):
                    g, kt = i // 4, i % 4
                    ktg, b_ktg = KTg[g % 2]
                    vg, b_vg = Vg[g % 2]
                    x_t, b_x = xt[i % 2]
                    xT_t, b_xT = xT[i % 4]
                    kf_t, b_kf = kf[i % 3]
                    ikf_t, b_ikf = ikf[i % 3]
                    kb_t, b_kb = kb[i % 2]
                    ikb_t, b_ikb = ikb[i % 2]
                    S.dma(x_t[:], xb[i * 128:(i + 1) * 128, :], writes=[b_x])
                    norm_transpose(x_t[:], b_x, 128, xh, b_xh, junk, b_junk, ssq[:, 0:1], b_ssq, i % 2,
                                   xT_t[:, :, 0:128], b_xT)
                    S.mark()
                    proj_tok(2, xT_t[:, :, 0:128], b_xT, CK, 512)
                    S.op("dve", lambda v: v.tensor_tensor(out=kf_t[:], in0=PB[2][:, :], in1=bias_row[:, CK:CK + 512], op=OP.add),
                         reads=[bPB[2], b_biasrow], writes=[b_kf])
                    proj_tok(3, xT_t[:, :, 0:128], b_xT, CV, 512)
                    S.op("dve", lambda v: v.tensor_tensor(
                        out=vg[:, kt, :].rearrange("p (h d) -> p h d", h=8)[:, :, 0:64],
                        in0=PB[3][:, :].rearrange("p (h d) -> p h d", h=8),
                        in1=bias_row[:, CV:CV + 512].rearrange("p (h d) -> p h d", h=8), op=OP.add),
                         reads=[bPB[3], b_biasrow], writes=[b_vg])
                    proj_tok(4, xT_t[:, :, 0:128], b_xT, CIK, 64)
                    S.op("dve", lambda v: v.tensor_tensor(out=ikf_t[:], in0=PB[4][:, 0:64], in1=bias_row[:, CIK:CIK + 64], op=OP.add),
                         reads=[bPB[4], b_biasrow], writes=[b_ikf])
                    if kt == 3:
                        S.dma(V_scr[g], vg[:], reads=[b_vg], writes=[b_Vs[g]])
                    S.mark()
                    headnorm_rope(kf_t[:, :], b_kf, gk_row, b_gk, csA[:, i, :], b_csA, kb_t[:, :], b_kb, tmp, b_tmp, st8, b_st8, 8, True,
                                  tmpr=tmpB, b_tmpr=b_tmpB, mark_mid=True)
                    headnorm_rope(ikf_t[:, :], b_ikf, None, None, csA[:, i, :], b_csA, ikb_t[:, 0:64], b_ikb, tmp, b_tmp, st8, b_st8, 1, False,
                                  tmpr=tmpB, b_tmpr=b_tmpB)
                    S.op("act", lambda a: a.activation(out=ikb_t[:, 64:128], in_=ikb_t[:, 0:64], func=AF.Copy), reads=[b_ikb], writes=[b_ikb])
                    S.mark()
                    pb2 = 5 + (i % 2)

                    def tr2(t):
                        for pr in range(4):
                            t.transpose(PBh[pb2][:, pr * 128:(pr + 1) * 128], kb_t[:, pr * 128:(pr + 1) * 128], ident_b[:])
                        return t.transpose(PBh[pb2][:, 512:640], ikb_t[:, :], ident_b[:])
                    S.op("pe", tr2, reads=[b_kb, b_ikb, b_identb], writes=[bPB[pb2]])
                    S.op("act", lambda a: a.activation(out=ktg[:, :, kt * 128:(kt + 1) * 128],
                                                       in_=PBh[pb2][:, 0:512].rearrange("p (k t) -> p k t", k=4), func=AF.Copy),
                         reads=[bPB[pb2]], writes=[b_ktg])
                    S.op("act", lambda a: a.activation(out=IKT2[:, i * 128:(i + 1) * 128], in_=PBh[pb2][:, 512:640], func=AF.Copy),
                         reads=[bPB[pb2]], writes=[b_IKT2])
                    if kt == 3:
                        S.dma(KT_scr[g], ktg[:], reads=[b_ktg], writes=[b_KTs[g]])

                with ExitStack() as pw:
                    wa1, b_wa1 = sbt(pw, "wadefer", [128, 8, 1024], F32)
                    wab1, b_wab1 = sbt(pw, "wab1", [128, 8, 1024], BF16)
                    sc_rep, b_screp = sbt(pw, "sc_rep2", [128, 8, 2, 128], BF16)
                    badar, b_badar = sbt(pw, "badar", [128, 2, 1024], F32)
                    screp_box["v"] = (sc_rep, b_screp, badar, b_badar)
                    S.dma(badar[:, 0, :], bada_row[2 * D:3 * D].partition_broadcast(128), writes=[b_badar])
                    S.dma(badar[:, 1, :], bada_row[5 * D:6 * D].partition_broadcast(128), reads=[], writes=[b_badar])
                    for kc in range(8):
                        S.op("dve", lambda v: v.tensor_copy(sc_rep[:, kc, :, :], c2[:, kc, :].unsqueeze(2).to_broadcast([128, 2, 128])),
                             reads=[b_c2], writes=[b_screp])
                    tiles1 = [S.record(lambda: p1_tile(i)) for i in range(NT)]
                    for i_, m_ in enumerate((2, 3, 4, 5)):
                        tiles1[4 * i_].append(S.record(lambda: mod_block(m_, wa1, b_wa1, wab1, b_wab1, 7, (7, 7)))[0])
                    p1_drain = S.pipeline(tiles1, hold_drain=True)
                    S.soft_barrier([b_wa1, b_wab1, b_screp, b_badar])
                dump("IKT2", IKT2[:], b_IKT2)

                with ExitStack() as p2:
                    dgc, b_dgc = sbt(p2, "dgc", [128, 4, 31, 128], BF16)
                    b_dgc2 = Buf("dgc2")
                    wdw_sb, b_wdw = sbt(p2, "wdw_sb", [128, 4, 31], F32)
                    bdw_sb, b_bdw = sbt(p2, "bdw_sb", [128, 4], F32)
                    gln_sb, b_gln = sbt(p2, "gln_sb", [128, 512], F32)
                    bln_sb, b_bln = sbt(p2, "bln_sb", [128, 512], F32)
                    xhl, b_xhl = sbt(p2, "xhl", [32, D], F32)
                    qf, b_qf = sbt(p2, "qf", [128, 512], F32)
                    qb, b_qb = sbt(p2, "qb", [128, 512], BF16)
                    qf2, b_qf2 = sbt(p2, "qf2", [128, 512], F32)
                    tmpq, b_tmpq = sbt(p2, "tmpq", [128, 512], F32)
                    st8q, b_st8q = sbt(p2, "st8q", [128, 8], F32)
                    tmpBq, b_tmpBq = sbt(p2, "tmpBq", [128, 128], F32)
                    qb2, b_qb2 = sbt(p2, "qb2", [128, 512], BF16)
                    qT, b_qT = sbt(p2, "qT", [128, 4, 128], BF16)
                    iqT, b_iqT = sbt(p2, "iqT", [128, 4, 128], BF16)
                    sgs = [sbt(p2, "sg%d" % i, [128, 160], F32) for i in range(2)]
                    uTs = [sbt(p2, "uT%d" % i, [128, 4, 160], BF16) for i in range(2)]
                    yT, b_yT = sbt(p2, "yT", [128, 4, 128], F32)
                    ysb = [sbt(p2, "ysb%d" % i, [128, 512], F32) for i in range(2)]
                    yn, b_yn = sbt(p2, "yn", [128, 512], F32)
                    bst, b_bst = sbt(p2, "bst", [128, 8], F32)
                    junkD, b_junkD = sbt(p2, "junkD", [128, 512], BF16)
                    sgD, b_sgD = sbt(p2, "sgD", [128, 512], F32)
                    stD, b_stD = sbt(p2, "stD", [128, 1], F32)
                    cvb, b_cvb = sbt(p2, "cvb", [128, 512], BF16)
                    mixc, b_mixcs = sbt(p2, "mixc", [128, 4, 128], BF16)
                    S.dma(wdw_sb[:], wdwT, writes=[b_wdw])
                    S.dma(bdw_sb[:], bdw_col, writes=[b_bdw])
                    S.dma(gln_sb[:], gln.partition_broadcast(128), writes=[b_gln])
                    S.dma(bln_sb[:], bln.partition_broadcast(128), writes=[b_bln])
                    def build_dgc():
                        def one(cc, j):
                            if (cc * 31 + j) % 3 == 0:
                                S.op("dve", lambda g: g.tensor_scalar(out=dgc[:, cc, j, :], in0=ident_f[:], scalar1=wdw_sb[:, cc, j:j + 1],
                                                                      scalar2=None, op0=OP.mult),
                                     reads=[b_identf, b_wdw], writes=[b_dgc])
                            else:
                                S.op("act", lambda a: a.activation(out=dgc[:, cc, j, :], in_=ident_f[:], func=AF.Copy,
                                                                   scale=wdw_sb[:, cc, j:j + 1]),
                                     reads=[b_identf, b_wdw], writes=[b_dgc2])
                        for cc in range(4):
                            for j in range(31):
                                one(cc, j)

                    def p2_slot(n):
                        x_t, b_x = xt[n % 2]
                        xT_t, b_xT = xT[n % 4]
                        y_t, b_ys = ysb[n % 2]
                        uT, b_uT = uTs[n % 2]
                        S.dma(x_t[:], xo[n * 128:(n + 1) * 128, :], writes=[b_x])
                        S.dma(xhl[:], xhal[n * 32:(n + 1) * 32, :], writes=[b_xhl])
                        norm_transpose(x_t[:], b_x, 128, xh, b_xh, junk, b_junk, ssq[:, 0:1], b_ssq, 0, xT_t[:, :, 32:160], b_xT)
                        norm_transpose(xhl[:], b_xhl, 32, xh, b_xh, junk, b_junk, ssq[0:32, 1:2], b_ssq, 0, xT_t[:, :, 0:32], b_xT)
                        S.mark()
                        own = xT_t[:, :, 32:160]
                        proj_tok(1, own, b_xT, CQ, 512)
                        S.op("dve", lambda v: v.tensor_tensor(out=qf[:], in0=PB[1][:, :], in1=bias_row[:, CQ:CQ + 512], op=OP.add),
                             reads=[bPB[1], b_biasrow], writes=[b_qf])
                        headnorm_rope(qf[:, :], b_qf, gq_row, b_gq, csO[:, n, :], b_csO, qb[:, :], b_qb, tmpq, b_tmpq, st8q, b_st8q, 8, True,
                                      tmpr=tmpBq, b_tmpr=b_tmpBq)

                        def trq(t):
                            last = None
                            for pr in range(4):
                                last = t.transpose(PBh[7][:, pr * 128:(pr + 1) * 128], qb[:, pr * 128:(pr + 1) * 128], ident_b[:])
                            return last
                        S.op("pe", trq, reads=[b_qb, b_identb], writes=[bPB[7]])
                        S.op("act", lambda a: a.activation(out=qT[:], in_=PBh[7][:, 0:512].rearrange("p (k t) -> p k t", k=4), func=AF.Copy),
                             reads=[bPB[7]], writes=[b_qT])
                        S.dma(qT_scr[n], qT[:], reads=[b_qT], writes=[b_qTs[n]])
                        S.mark()
                        proj_tok(3, own, b_xT, CIQ, 512)
                        S.op("dve", lambda v: v.tensor_tensor(out=qf2[:], in0=PB[3][:, :], in1=bias_row[:, CIQ:CIQ + 512], op=OP.add),
                             reads=[bPB[3], b_biasrow], writes=[b_qf2])
                        headnorm_rope(qf2[:, :], b_qf2, None, None, csO[:, n, :], b_csO, qb2[:, :], b_qb2, tmp, b_tmp, st8, b_st8, 8, False,
                                      tmpr=tmpC, b_tmpr=b_tmpC)

                        def triq(t):
                            last = None
                            for pr in range(4):
                                last = t.transpose(PBh[7][:, 512 + pr * 128:512 + (pr + 1) * 128], qb2[:, pr * 128:(pr + 1) * 128], ident_b[:])
                            return last
                        S.op("pe", triq, reads=[b_qb2, b_identb], writes=[bPB[7]])
                        S.op("act", lambda a: a.activation(out=iqT[:], in_=PBh[7][:, 512:1024].rearrange("p (k t) -> p k t", k=4), func=AF.Copy),
                             reads=[bPB[7]], writes=[b_iqT])
                        S.dma(iqT_scr[n], iqT[:], reads=[b_iqT], writes=[b_iqTs[n]])
                        proj_tok(3, own, b_xT, CIK, 72)
                        S.op("dve", lambda v: v.tensor_tensor(out=iw_all[:, n, :], in0=PB[3][:, 64:72], in1=bias_row[:, CIW:CIW + 8], op=OP.add),
                             reads=[bPB[3], b_biasrow], writes=[b_iw])
                        S.op("dve", lambda v: v.tensor_scalar(out=iw_all[:, n, :], in0=iw_all[:, n, :], scalar1=float(8 ** -0.5 * 64 ** -0.5),
                                                              scalar2=None, op0=OP.mult), reads=[b_iw], writes=[b_iw])
                        S.mark()
                        for cc_ in range(4):
                            def conv_in(cc):
                                pbi = 4
                                sg_t, b_sg_t = sgs[cc % 2]

                                def mmc(t):
                                    last = None
                                    for half, c0 in ((0, CCA), (1, CCG)):
                                        for kc in range(8):
                                            last = t.matmul(PB[pbi][:, half * 160:(half + 1) * 160],
                                                            lhsT=Wp[:, kc, c0 + cc * 128:c0 + (cc + 1) * 128], rhs=xT_t[:, kc, :],
                                                            start=(kc == 0), stop=(kc == 7))
                                    return last
                                S.op("pe", mmc, reads=[b_xT, b_Wp], writes=[bPB[pbi]])
                                S.op("act", lambda a: a.activation(out=sg_t[:], in_=PB[pbi][:, 160:320], func=AF.Sigmoid,
                                                                   bias=bias_cc[:, 4 + cc:5 + cc]),
                                     reads=[bPB[pbi], b_biascc], writes=[b_sg_t])
                                S.op("dve", lambda v: v.scalar_tensor_tensor(out=uT[:, cc, :], in0=PB[pbi][:, 0:160], scalar=bias_cc[:, cc:cc + 1],
                                                                             in1=sg_t[:], op0=OP.add, op1=OP.mult),
                                     reads=[bPB[pbi], b_biascc, b_sg_t], writes=[b_uT])
                            conv_in(cc_)
                        S.op("dve", lambda v: v.tensor_scalar(out=uT[:, :, 0:32], in0=uT[:, :, 0:32], scalar1=hfl_sb[:, n:n + 1],
                                                              scalar2=None, op0=OP.mult), reads=[b_uT, b_hfl], writes=[b_uT])
                        S.mark()
                        for cc_ in range(4):
                            def conv_dw(cc):
                                pbo = (PB[5], bPB[5])

                                def mcv(t):
                                    last = None
                                    for j in range(31):
                                        last = t.matmul(pbo[0][:, 0:128], lhsT=dgc[:, cc, j, :], rhs=uT[:, cc, 2 + j:2 + j + 128],
                                                        start=(j == 0), stop=(j == 30))
                                    return last
                                S.op("pe", mcv, reads=[b_uT, b_dgc, b_dgc2], writes=[pbo[1]])
                                S.op("act", lambda a: a.activation(out=yT[:, cc, :], in_=pbo[0][:, 0:128], func=AF.Identity,
                                                                   bias=bdw_sb[:, cc:cc + 1]),
                                     reads=[pbo[1], b_bdw], writes=[b_yT])
                            conv_dw(cc_)

                        def try_(t):
                            last = None
                            for cc in range(4):
                                last = t.transpose(PB[6][:, cc * 128:(cc + 1) * 128], yT[:, cc, :], ident_f[:])
                            return last
                        S.op("pe", try_, reads=[b_yT, b_identf], writes=[bPB[6]])
                        S.op("dve", lambda v: v.tensor_copy(y_t[:], PB[6][:, :]), reads=[bPB[6]], writes=[b_ys])
                        S.mark()
                        S.op("dve", lambda v: v.bn_stats(out=bst[:, 0:6], in_=y_t[:]), reads=[b_ys], writes=[b_bst])
                        S.op("dve", lambda v: v.bn_aggr(out=bst[:, 6:8], in_=bst[:, 0:6]), reads=[b_bst], writes=[b_bst])
                        S.op("dve", lambda v: v.tensor_scalar(out=bst[:, 7:8], in0=bst[:, 7:8], scalar1=EPS, scalar2=None, op0=OP.add),
                             reads=[b_bst], writes=[b_bst])
                        S.op("pool", lambda g: g.tensor_tensor(out=bst[:, 7:8], in0=bst[:, 7:8], in1=mhalf[:, 0:1], op=OP.pow),
                             reads=[b_bst, b_mhalf], writes=[b_bst])
                        S.op("dve", lambda v: v.tensor_scalar(out=yn[:], in0=y_t[:], scalar1=bst[:, 6:7], scalar2=bst[:, 7:8],
                                                              op0=OP.subtract, op1=OP.mult), reads=[b_ys, b_bst], writes=[b_yn])
                        S.op("pool", lambda g: g.tensor_tensor(out=yn[:], in0=yn[:], in1=gln_sb[:], op=OP.mult), reads=[b_yn, b_gln], writes=[b_yn])
                        S.op("pool", lambda g: g.tensor_tensor(out=yn[:], in0=yn[:], in1=bln_sb[:], op=OP.add), reads=[b_yn, b_bln], writes=[b_yn])
                        S.op("act", lambda a: a.activation(out=sgD[:], in_=yn[:], func=AF.Sigmoid), reads=[b_yn], writes=[b_sgD])
                        S.op("pool", lambda g: g.tensor_tensor(out=yn[:], in0=yn[:], in1=sgD[:], op=OP.mult), reads=[b_yn, b_sgD], writes=[b_yn])
                        S.op("act", lambda a: a.activation(out=junkD[:], in_=yn[:], func=AF.Square, accum_out=stD[:, 0:1]),
                             reads=[b_yn], writes=[b_junkD, b_stD])
                        rstd_from_ssq(stD[:, 0:1], b_stD, 128, 1.0 / 512)
                        S.op("act", lambda a: a.activation(out=cvb[:], in_=yn[:], func=AF.Copy, scale=stD[:, 0:1]),
                             reads=[b_yn, b_stD], writes=[b_cvb])

                        def trc(t):
                            last = None
                            for pr in range(4):
                                last = t.transpose(PBh[2][:, pr * 128:(pr + 1) * 128], cvb[:, pr * 128:(pr + 1) * 128], ident_b[:])
                            return last
                        S.op("pe", trc, reads=[b_cvb, b_identb], writes=[bPB[2]])
                        S.op("act", lambda a: a.activation(out=mixc[:], in_=PBh[2][:, 0:512].rearrange("p (k t) -> p k t", k=4), func=AF.Copy),
                             reads=[bPB[2]], writes=[b_mixcs])
                        S.dma(mix_scr[n][:, 4:8, :], mixc[:], reads=[b_mixcs], writes=[b_mixc[n]])

                    tiles2 = [S.record(lambda: p2_slot(n)) for n in range(NSLOT)]
                    dg_ops = S.record(build_dgc)[0]
                    for i_ in range(3):
                        part = dg_ops[i_::3]
                        base = tiles2[i_][0]
                        merged = []
                        for q_ in range(max(len(part), len(base))):
                            if q_ < len(base):
                                merged.append(base[q_])
                            merged.extend(part[4 * q_:4 * q_ + 4])
                        merged.extend(part[4 * max(len(part), len(base)):])
                        tiles2[i_][0] = merged
                    S.pipeline(tiles2, lead=p1_drain)
                    dump("iw_all", iw_all[:], b_iw)
                S.barrier()
            p012.close()

            with ExitStack() as p3:
                score = [sbt(p3, "score%d" % i, [128, S_LEN], F32) for i in range(2)]
                Mall = [sbt(p3, "Mall%d" % i, [128, S_LEN], BF16) for i in range(2)]
                junk8, b_junk8 = sbt(p3, "junk8", [128, S_LEN], U8)
                iota, b_iota = sbt(p3, "iota", [128, 512], F32)
                pen, b_pen = sbt(p3, "pen", [128, 512], F32)
                qT3 = [sbt(p3, "qT3_%d" % i, [128, 4, 128], BF16) for i in range(2)]
                iqT3 = [sbt(p3, "iqT3_%d" % i, [128, 4, 128], BF16) for i in range(2)]
                dgw, b_dgw = sbt(p3, "dgw", [128, 8, 128], BF16)
                rl = [sbt(p3, "rl%d" % i, [128, 512], BF16) for i in range(4)]
                bis = [sbt(p3, "bis%d" % i, [128, 12], F32) for i in range(2)]
                mk, b_mk = sbt(p3, "mk", [128, 4096], F32)
                t16, b_t16 = sbt(p3, "t16", [128, 40], F32)
                iota8, b_iota8 = sbt(p3, "iota8", [128, 8], F32)
                R4, b_R4 = sbt(p3, "R4", [128, 512], BF16)
                KT3 = [sbt(p3, "KT3_%d" % i, [128, 4, 512], BF16) for i in range(2)]
                V3 = [sbt(p3, "V3_%d" % i, [128, 4, 520], BF16) for i in range(2)]
                Ex = [sbt(p3, "Ex%d" % i, [128, 512], BF16) for i in range(4)]
                LB = [0, 1, 4, 5]
                SKEW = 3
                rden, b_rden = sbt(p3, "rden", [128, 8], F32)
                attn, b_attn = sbt(p3, "attn", [128, 512], F32)
                ajunk, b_ajunk = sbt(p3, "ajunk", [128, 512], BF16)
                ast, b_ast = sbt(p3, "ast", [128, 1], F32)
                ab, b_ab = sbt(p3, "ab", [128, 512], BF16)
                mixa, b_mixas = sbt(p3, "mixa", [128, 4, 128], BF16)
                S.op("pool", lambda g: g.iota(iota[:], pattern=[[1, 512]], base=0, channel_multiplier=0,
                                              allow_small_or_imprecise_dtypes=True), writes=[b_iota])
                S.op("pool", lambda g: g.iota(iota8[:], pattern=[[1, 8]], base=0, channel_multiplier=0,
                                              allow_small_or_imprecise_dtypes=True), writes=[b_iota8])
                for r_ in range(4):
                    S.op("pool", lambda g: g.tensor_copy(R4[:, r_ * 128:(r_ + 1) * 128], ident_b[:]), reads=[b_identb], writes=[b_R4])

                glist = [(n, g) for n in range(NSLOT) for g in range(slot_ngroups(n))]
                gpos = {ng_: k for k, ng_ in enumerate(glist)}

                def kv_dma(k):
                    if k >= len(glist):
                        return
                    n_, g_ = glist[k]
                    S.dma(KT3[k % 2][0][:], KT_scr[g_], reads=[b_KTs[g_]], writes=[KT3[k % 2][1]])
                    S.dma(V3[k % 2][0][:], V_scr[g_], reads=[b_Vs[g_]], writes=[V3[k % 2][1]])

                def prep(n):
                    q_t, b_q = qT3[n % 2]
                    iq_t, b_iq = iqT3[n % 2]
                    S.dma(iq_t[:], iqT_scr[n], reads=[b_iqTs[n]], writes=[b_iq])
                    S.dma(q_t[:], qT_scr[n], reads=[b_qTs[n]], writes=[b_q])
                    for h in range(8):
                        S.op("dve", lambda v: v.tensor_scalar(out=dgw[:, h, :], in0=ident_f[:], scalar1=iw_all[:, n, h:h + 1],
                                                              scalar2=None, op0=OP.mult),
                             reads=[b_identf, b_iw], writes=[b_dgw])
                    S.op("dve", lambda v: v.tensor_scalar(out=pen[:], in0=iota[:], scalar1=lim_sb[:, n:n + 1], scalar2=NEG,
                                                          op0=OP.is_ge, op1=OP.mult), reads=[b_iota, b_lim], writes=[b_pen])

                def idx(n):
                    ng = slot_ngroups(n)
                    iq_t, b_iq = iqT3[n % 2]
                    sc_t, b_sc = score[n % 2]
                    pitems = [(g, j) for g in range(ng) for j in range(4)]

                    def zpair(pi):
                        g, j = pitems[pi]
                        b0, b1 = LB[2 * (pi % 2)], LB[2 * (pi % 2) + 1]

                        def f(t):
                            t.matmul(PB[b0][:, :], lhsT=iq_t[0:64, j, :], rhs=IKT2[0:64, g * 512:(g + 1) * 512], start=True, stop=True)
                            return t.matmul(PB[b1][:, :], lhsT=iq_t[64:128, j, :], rhs=IKT2[64:128, g * 512:(g + 1) * 512],
                                            start=True, stop=True)
                        S.op("pe", f, reads=[b_iq, b_IKT2], writes=[bPB[b0], bPB[b1]])
                    zpair(0)
                    for pi, (g, j) in enumerate(pitems):
                        if pi + 1 < len(pitems):
                            zpair(pi + 1)
                        sb_ = 2 + (g % 2)
                        for e in range(2):
                            h = 2 * j + e
                            zb = LB[2 * (pi % 2) + e]
                            r_t, b_r = rl[(2 * pi + e) % 4]
                            S.op("act", lambda a: a.activation(out=r_t[:], in_=PB[zb][:, :], func=AF.Relu),
                                 reads=[bPB[zb]], writes=[b_r])
                            S.op("pe", lambda t: t.matmul(PB[sb_][:, :], lhsT=dgw[:, h, :], rhs=r_t[:], start=(h == 0), stop=(h == 7)),
                                 reads=[b_dgw, b_r], writes=[bPB[sb_]])
                        if j == 3:
                            if g == ng - 1:
                                S.op("dve", lambda v: v.tensor_tensor(out=sc_t[:, g * 512:(g + 1) * 512], in0=PB[sb_][:, :], in1=pen[:], op=OP.add),
                                     reads=[bPB[sb_], b_pen], writes=[b_sc])
                            else:
                                S.op("act", lambda a: a.activation(out=sc_t[:, g * 512:(g + 1) * 512], in_=PB[sb_][:, :], func=AF.Copy),
                                     reads=[bPB[sb_]], writes=[b_sc])

                def bisect(n):
                    nk = slot_ngroups(n) * 512
                    sc_t, b_sc = score[n % 2]
                    bs, b_bs = bis[n % 2]
                    S.op("dve", lambda v: v.memset(bs[:, 0:1], 0.0), writes=[b_bs])
                    for k in range(NBIS):
                        S.op("dve", lambda v: v.tensor_scalar(out=junk8[:, 0:nk], in0=sc_t[:, 0:nk], scalar1=bs[:, 0:1], scalar2=0.0,
                                                              op0=OP.is_ge, op1=OP.add, accum_out=bs[:, 1:2]),
                             reads=[b_sc, b_bs], writes=[b_junk8, b_bs])
                        wk = BIS_R / (2 ** k)
                        if k < NBIS - 1:
                            wn = wk / 2
                            S.op("dve", lambda v: v.tensor_scalar(out=bs[:, 2:3], in0=bs[:, 1:2], scalar1=255.5, scalar2=2 * wn,
                                                                  op0=OP.is_gt, op1=OP.mult), reads=[b_bs], writes=[b_bs])
                            S.op("dve", lambda v: v.scalar_tensor_tensor(out=bs[:, 0:1], in0=bs[:, 2:3], scalar=-wn, in1=bs[:, 0:1],
                                                                         op0=OP.add, op1=OP.add), reads=[b_bs], writes=[b_bs])
                        else:
                            S.op("dve", lambda v: v.tensor_scalar(out=bs[:, 2:3], in0=bs[:, 1:2], scalar1=255.5, scalar2=wk,
                                                                  op0=OP.is_gt, op1=OP.mult), reads=[b_bs], writes=[b_bs])
                            S.op("dve", lambda v: v.scalar_tensor_tensor(out=bs[:, 4:5], in0=bs[:, 2:3], scalar=-wk, in1=bs[:, 0:1],
                                                                         op0=OP.add, op1=OP.add), reads=[b_bs], writes=[b_bs])
                            S.op("dve", lambda v: v.tensor_scalar(out=bs[:, 5:6], in0=bs[:, 4:5], scalar1=wk, scalar2=None, op0=OP.add),
                                 reads=[b_bs], writes=[b_bs])
                    S.op("dve", lambda v: v.tensor_scalar(out=junk8[:, 0:nk], in0=sc_t[:, 0:nk], scalar1=bs[:, 5:6], scalar2=0.0,
                                                          op0=OP.is_ge, op1=OP.add, accum_out=bs[:, 6:7]),
                         reads=[b_sc, b_bs], writes=[b_junk8, b_bs])
                    S.op("dve", lambda v: v.tensor_scalar(out=bs[:, 7:8], in0=bs[:, 6:7], scalar1=-1.0, scalar2=255.0,
                                                          op0=OP.mult, op1=OP.add), reads=[b_bs], writes=[b_bs])
                    npc = (nk + 4095) // 4096
                    for pc in range(npc):
                        c0, c1 = pc * 4096, min(nk, (pc + 1) * 4096)
                        S.op("dve", lambda v: v.scalar_tensor_tensor(out=mk[:, 0:c1 - c0], in0=junk8[:, c0:c1], scalar=NEG, in1=sc_t[:, c0:c1],
                                                                     op0=OP.mult, op1=OP.add), reads=[b_junk8, b_sc], writes=[b_mk])
                        S.op("dve", lambda v: v.max(out=t16[:, pc * 8:(pc + 1) * 8], in_=mk[:, 0:c1 - c0]), reads=[b_mk], writes=[b_t16])
                    if npc == 2:
                        S.op("dve", lambda v: v.max(out=t16[:, 16:24], in_=t16[:, 0:16]), reads=[b_t16], writes=[b_t16])
                        t8 = t16[:, 16:24]
                    else:
                        t8 = t16[:, 0:8]
                    S.op("dve", lambda v: v.tensor_scalar(out=t16[:, 24:32], in0=t8, scalar1=bs[:, 4:5], scalar2=None, op0=OP.subtract),
                         reads=[b_t16, b_bs], writes=[b_t16])
                    S.op("dve", lambda v: v.scalar_tensor_tensor(out=t16[:, 32:40], in0=iota8[:], scalar=bs[:, 7:8], in1=t16[:, 24:32],
                                                                 op0=OP.is_equal, op1=OP.mult, accum_out=bs[:, 8:9]),
                         reads=[b_iota8, b_t16, b_bs], writes=[b_t16, b_bs])
                    S.op("dve", lambda v: v.tensor_tensor(out=bs[:, 3:4], in0=bs[:, 8:9], in1=bs[:, 4:5], op=OP.add),
                         reads=[b_bs], writes=[b_bs])
                    S.op("dve", lambda v: v.tensor_tensor(out=bs[:, 3:4], in0=bs[:, 3:4], in1=ovr_sb[:, n:n + 1], op=OP.min),
                         reads=[b_bs, b_ovr], writes=[b_bs])
                    m_t, b_m = Mall[n % 2]
                    S.op("dve", lambda v: v.tensor_scalar(out=m_t[:, 0:nk], in0=sc_t[:, 0:nk], scalar1=bs[:, 3:4],
                                                          scalar2=-30000.0, op0=OP.is_lt, op1=OP.mult),
                         reads=[b_sc, b_bs], writes=[b_m])
                    if n == 1:
                        dump("score1", sc_t[:], b_sc)
                        dump("bis1", bs[:], b_bs)

                def att_main(n):
                    ng = slot_ngroups(n)
                    q_t, b_q = qT3[n % 2]
                    m_t, b_m = Mall[n % 2]
                    units = [(g, kt) for g in range(ng) for kt in range(4)]

                    def lunit(ui):
                        g, kt = units[ui]
                        k = gpos[(n, g)]
                        kt_t, b_kt = KT3[k % 2]
                        bA, bB = LB[2 * (ui % 2)], LB[2 * (ui % 2) + 1]
                        c0 = g * 512 + kt * 128

                        def mml(t):
                            for j in range(4):
                                t.matmul(PB[bA][:, j * 128:(j + 1) * 128], lhsT=kt_t[0:64, j, kt * 128:(kt + 1) * 128],
                                         rhs=q_t[0:64, j, :], start=(j == 0), stop=False, skip_group_check=True)
                                t.matmul(PB[bB][:, j * 128:(j + 1) * 128], lhsT=kt_t[64:128, j, kt * 128:(kt + 1) * 128],
                                         rhs=q_t[64:128, j, :], start=(j == 0), stop=False, skip_group_check=True)
                            t.matmul(PB[bA][:, :], lhsT=m_t[:, c0:c0 + 128], rhs=R4[:], start=False, stop=True, skip_group_check=True)
                            return t.matmul(PB[bB][:, :], lhsT=m_t[:, c0:c0 + 128], rhs=R4[:], start=False, stop=True,
                                            skip_group_check=True)
                        S.op("pe", mml, reads=[b_kt, b_q, b_m, b_R4], writes=[bPB[bA], bPB[bB]])
                    lunit(0)
                    for ui, (g, kt) in enumerate(units):
                        k = gpos[(n, g)]
                        v_t, b_v = V3[k % 2]
                        if ui + 1 < len(units):
                            lunit(ui + 1)
                        for par_ in range(2):
                            def half(par):
                                pl = LB[2 * (ui % 2) + par]
                                e_t, b_e = Ex[2 * (ui % 2) + par]
                                S.op("act", lambda a: a.activation(out=e_t[:], in_=PB[pl][:, :], func=AF.Exp, scale=0.125),
                                     reads=[bPB[pl]], writes=[b_e])

                                def mmo(t):
                                    last = None
                                    for j in range(4):
                                        h = 2 * j + par
                                        po = 6 + h // 4
                                        hc = (h % 4) * 65
                                        first = (g == 0 and kt == 0 and h % 4 == 0)
                                        lastf = (g == ng - 1 and kt == 3 and h % 4 == 3)
                                        last = t.matmul(PB[po][:, hc:hc + 65], lhsT=e_t[:, j * 128:(j + 1) * 128],
                                                        rhs=v_t[:, kt, h * 65:(h + 1) * 65], start=first, stop=lastf, skip_group_check=True)
                                    return last
                                S.op("pe", mmo, reads=[b_e, b_v], writes=[bPB[6], bPB[7]])
                            half(par_)
                        if kt == 3:
                            kv_dma(k + 2)

                def final_a(n):
                    for b in range(2):
                        pv = PB[6 + b][:, 0:260].rearrange("p (h d) -> p h d", h=4)
                        S.op("dve", lambda v: v.reciprocal(rden[:, b * 4:(b + 1) * 4].unsqueeze(2), pv[:, :, 64:65]),
                             reads=[bPB[6 + b]], writes=[b_rden])
                        S.op("dve", lambda v: v.tensor_tensor(out=attn[:, b * 256:(b + 1) * 256].rearrange("p (h d) -> p h d", h=4),
                                                              in0=pv[:, :, 0:64],
                                                              in1=rden[:, b * 4:(b + 1) * 4].unsqueeze(2).to_broadcast([128, 4, 64]),
                                                              op=OP.mult), reads=[bPB[6 + b], b_rden], writes=[b_attn])
                    if n == 1:
                        dump("attn1", attn[:], b_attn)

                def final_b1(n):
                    S.op("act", lambda a: a.activation(out=ajunk[:], in_=attn[:], func=AF.Square, accum_out=ast[:, 0:1]),
                         reads=[b_attn], writes=[b_ajunk, b_ast])
                    rstd_from_ssq(ast[:, 0:1], b_ast, 128, 1.0 / 512)
                    S.op("act", lambda a: a.activation(out=ab[:], in_=attn[:], func=AF.Copy, scale=ast[:, 0:1]),
                         reads=[b_attn, b_ast], writes=[b_ab])

                def final_b2(n):
                    def tra(t):
                        last = None
                        for pr in range(4):
                            last = t.transpose(PBh[3][:, pr * 128:(pr + 1) * 128], ab[:, pr * 128:(pr + 1) * 128], ident_b[:])
                        return last
                    S.op("pe", tra, reads=[b_ab, b_identb], writes=[bPB[3]])
                    S.op("act", lambda a: a.activation(out=mixa[:], in_=PBh[3][:, 0:512].rearrange("p (k t) -> p k t", k=4), func=AF.Copy),
                         reads=[bPB[3]], writes=[b_mixas])
                    S.dma(mix_scr[n][:, 0:4, :], mixa[:], reads=[b_mixas], writes=[b_mixa[n]])

                kv_dma(0)
                kv_dma(1)
                prep(0)
                idx(0)
                for n in range(NSLOT):
                    if n + 1 < NSLOT:
                        prep(n + 1)
                    bisect(n)
                    if n >= 1:
                        final_a(n - 1)
                    if n + 1 < NSLOT:
                        idx(n + 1)
                    if n >= 1:
                        final_b1(n - 1)
                    att_main(n)
                    if n >= 1:
                        final_b2(n - 1)
                final_a(NSLOT - 1)
                final_b1(NSLOT - 1)
                final_b2(NSLOT - 1)
                S.barrier()
        S.barrier()

        with ExitStack() as eb:
            Wo, b_Wo = sbt(eb, "Wo", [128, 8, D], BF16)
            W1, b_W1 = sbt(eb, "W1", [128, 8, 4 * D], BF16)
            W2, b_W2 = sbt(eb, "W2", [128, 32, D], BF16)
            b1c, b_b1c = sbt(eb, "b1c", [128, 32], F32)
            S.op("pool", lambda g: g.memset(b1c[:], 0.0), writes=[b_b1c])
            with ExitStack() as ew:
                NSTG = 4
                stg = [sbt(ew, "stg%d" % i, [128, 2 * D], F32) for i in range(NSTG)]
                si = 0
                for kq in range(4):
                    st, b_st = stg[si % NSTG]
                    si += 1
                    S.dma(st[:].rearrange("p (k n) -> p k n", k=2), w_out[kq * 256:(kq + 1) * 256, :].rearrange("(k p) n -> p k n", p=128),
                          writes=[b_st])
                    for k2 in range(2):
                        kc = kq * 2 + k2
                        S.op("dve", lambda v: v.scalar_tensor_tensor(out=Wo[:, kc, :], in0=st[:, k2 * D:(k2 + 1) * D], scalar=gcat[:, kc:kc + 1],
                                                                     in1=gt_row[:, 0, :], op0=OP.mult, op1=OP.mult),
                             reads=[b_st, b_gcat, b_gtrow], writes=[b_Wo])
                def w1_chunk(kc, hf):
                    nonlocal_si[0] += 1
                    st, b_st = stg[nonlocal_si[0] % NSTG]
                    S.dma(st[:], w_ff1[kc * 128:(kc + 1) * 128, hf * 2048:(hf + 1) * 2048], writes=[b_st])

                    def mmb(t):
                        last = None
                        for fc in range(16):
                            last = t.matmul(PB[7][:, fc:fc + 1], lhsT=st[:, fc * 128:(fc + 1) * 128], rhs=modcol[:, 2, kc:kc + 1],
                                            start=True, stop=True)
                        return last
                    S.op("pe", mmb, reads=[b_st, b_modcol], writes=[bPB[7]])
                    S.op("dve", lambda v: v.tensor_tensor(out=b1c[:, hf * 16:(hf + 1) * 16], in0=PB[7][:, 0:16],
                                                          in1=b1c[:, hf * 16:(hf + 1) * 16], op=OP.add),
                         reads=[bPB[7], b_b1c], writes=[b_b1c])
                    S.op("act", lambda a: a.activation(out=W1[:, kc, hf * 2048:(hf + 1) * 2048], in_=st[:], func=AF.Copy,
                                                       scale=gmod[:, 1, kc:kc + 1]),
                         reads=[b_st, b_gmod], writes=[b_W1])

                def w2_chunk(fq):
                    nonlocal_si[0] += 1
                    st, b_st = stg[nonlocal_si[0] % NSTG]
                    S.dma(st[:].rearrange("p (f n) -> p f n", f=2), w_ff2[fq * 256:(fq + 1) * 256, :].rearrange("(f p) n -> p f n", p=128),
                          writes=[b_st])
                    for f_ in range(2):
                        def sc(f):
                            eng = "pool" if f % 2 == 0 else "dve"
                            S.op(eng, lambda g: g.tensor_tensor(out=W2[:, fq * 2 + f, :], in0=st[:, f * D:(f + 1) * D], in1=gt_row[:, 1, :],
                                                                op=OP.mult), reads=[b_st, b_gtrow], writes=[b_W2])
                        sc(f_)
                nonlocal_si = [si]
                for c_ in range(16):
                    w1_chunk(c_ // 2, c_ % 2)
                    w2_chunk(c_)
                dump("b1c", b1c[:], b_b1c)
                S.barrier()
            mixT = [sbt(eb, "mixT%d" % i, [128, 8, 128], BF16) for i in range(1)]
            xt4 = [sbt(eb, "xt4_%d" % i, [128, D], F32) for i in range(1)]
            x1 = [sbt(eb, "x1_%d" % i, [128, D], F32) for i in range(3)]
            xh4, b_xh4 = sbt(eb, "xh4", [128, D], BF16)
            ssq4, b_ssq4 = sbt(eb, "ssq4", [128, 1], F32)
            x1T = [sbt(eb, "x1T%d" % i, [128, 8, 128], BF16) for i in range(2)]
            rr = [sbt(eb, "rr%d" % i, [128, 512], F32) for i in range(2)]
            aTs = [sbt(eb, "aT%d" % i, [128, 32, 128], BF16) for i in range(2)]
            yo = [sbt(eb, "yo%d" % i, [128, D], F32) for i in range(1)]

            def p4_slot(n):
                m_t, b_m = mixT[0]
                x_t, b_x = xt4[0]
                x1_t, b_x1 = x1[n % 3]
                x1T_t, b_x1T = x1T[n % 2]
                yo_t, b_yo = yo[0]
                aT, b_aT = aTs[n % 2]
                S.dma(m_t[:], mix_scr[n], reads=[b_mixa[n], b_mixc[n]], writes=[b_m])
                S.dma(x_t[:], xo[n * 128:(n + 1) * 128, :], writes=[b_x])
                for cg_ in range(2):
                    def oproj(cg):
                        def mm1(t):
                            last = None
                            for kc in range(8):
                                last = t.matmul(PB[cg][:, :], lhsT=m_t[:, kc, :], rhs=Wo[:, kc, cg * 512:(cg + 1) * 512],
                                                start=(kc == 0), stop=(kc == 7))
                            return last
                        S.op("pe", mm1, reads=[b_m, b_Wo], writes=[bPB[cg]])
                        S.op("dve", lambda v: v.tensor_tensor(out=x1_t[:, cg * 512:(cg + 1) * 512], in0=PB[cg][:, :],
                                                              in1=x_t[:, cg * 512:(cg + 1) * 512], op=OP.add),
                             reads=[bPB[cg], b_x], writes=[b_x1])
                    oproj(cg_)
                S.op("act", lambda a: a.activation(out=xh4[:], in_=x1_t[:], func=AF.Square, accum_out=ssq4[:, 0:1]),
                     reads=[b_x1], writes=[b_xh4, b_ssq4])
                rstd_from_ssq(ssq4[:, 0:1], b_ssq4, 128, 1.0 / D)
                S.op("act", lambda a: a.activation(out=xh4[:], in_=x1_t[:], func=AF.Copy, scale=ssq4[:, 0:1]),
                     reads=[b_x1, b_ssq4], writes=[b_xh4])

                def tr4(t):
                    last = None
                    for kc in range(8):
                        last = t.transpose(PBh[2][:, kc * 128:(kc + 1) * 128], xh4[:, kc * 128:(kc + 1) * 128], ident_b[:])
                    return last
                S.op("pe", tr4, reads=[b_xh4, b_identb], writes=[bPB[2]])
                S.op("dve", lambda v: v.tensor_copy(x1T_t[:], PBh[2][:, :].rearrange("p (k t) -> p k t", k=8)),
                     reads=[bPB[2]], writes=[b_x1T])
                S.mark()
                for fq_ in range(8):
                    def ff1(fq):
                        ph = 3 + (fq % 2)
                        r_t, b_r = rr[fq % 2]

                        def mmh(t):
                            last = None
                            for f in range(4):
                                fc = fq * 4 + f
                                for kc in range(8):
                                    last = t.matmul(PB[ph][:, f * 128:(f + 1) * 128], lhsT=W1[:, kc, fc * 128:(fc + 1) * 128],
                                                    rhs=x1T_t[:, kc, :], start=(kc == 0), stop=(kc == 7))
                            return last
                        S.op("pe", mmh, reads=[b_x1T, b_W1], writes=[bPB[ph]])

                        def relu4(a):
                            last = None
                            for f in range(4):
                                fc = fq * 4 + f
                                last = a.activation(out=r_t[:, f * 128:(f + 1) * 128], in_=PB[ph][:, f * 128:(f + 1) * 128], func=AF.Relu,
                                                    bias=b1c[:, fc:fc + 1])
                            return last
                        S.op("act", relu4, reads=[bPB[ph], b_b1c], writes=[b_r])
                        S.op("pool", lambda v: v.tensor_tensor(out=aT[:, fq * 4:(fq + 1) * 4, :], in0=r_t[:].rearrange("p (f t) -> p f t", f=4),
                                                               in1=r_t[:].rearrange("p (f t) -> p f t", f=4), op=OP.mult),
                             reads=[b_r], writes=[b_aT])
                    ff1(fq_)
                S.mark()
                for cg_ in range(2):
                    def ff2(cg):
                        def mm2(t):
                            last = None
                            for fc in range(32):
                                last = t.matmul(PB[5 + cg][:, :], lhsT=aT[:, fc, :], rhs=W2[:, fc, cg * 512:(cg + 1) * 512],
                                                start=(fc == 0), stop=(fc == 31))
                            return last
                        S.op("pe", mm2, reads=[b_aT, b_W2], writes=[bPB[5 + cg]])
                        S.op("dve", lambda v: v.tensor_tensor(out=yo_t[:, cg * 512:(cg + 1) * 512], in0=PB[5 + cg][:, :],
                                                              in1=x1_t[:, cg * 512:(cg + 1) * 512], op=OP.add),
                             reads=[bPB[5 + cg], b_x1], writes=[b_yo])
                    ff2(cg_)
                S.dma(y[n * 128:(n + 1) * 128, :], yo_t[:], reads=[b_yo], writes=[b_y[n]])
                if n == 0:
                    dump("x1_0", x1_t[:], b_x1)

            S.pipeline([S.record(lambda: p4_slot(n)) for n in range(NSLOT)])
            S.barrier()
    return nc


def _rope_tables():
    pos = np.arange(S_LEN, dtype=np.float32)
    inv = (np.float32(500000.0) ** (-np.arange(0, 16, 2, dtype=np.float32) / np.float32(16))).astype(np.float32)
    ang = (pos[:, None] * inv[None, :]).astype(np.float32)
    return np.concatenate([np.cos(ang), np.sin(ang)], axis=1).astype(np.float32)


def _col(v, nchunk):
    return np.ascontiguousarray(np.asarray(v, np.float32).reshape(nchunk, 128).T)


def make_in_maps(x, c, w_ada, b_ada, g_norm1, w_in, g_q, g_k, w_dw, b_dw, g_conv_ln, b_conv_ln,
                 g_out_attn, g_out_conv, w_out, g_norm2, w_ff1, w_ff2):
    f = lambda a: np.ascontiguousarray(np.asarray(a, np.float32))
    x = f(x)
    cs = _rope_tables()
    shared = {
        "w_ada": f(w_ada[0]), "bada_col": _col(b_ada[0], 48), "bada_row": f(b_ada[0]),
        "g1col": _col(g_norm1[0], 8), "g2col": _col(g_norm2[0], 8), "w_in": f(w_in[0]),
        "gq": f(g_q[0]), "gk": f(g_k[0]),
        "wdwT": np.ascontiguousarray(f(w_dw[0]).T.reshape(4, 128, 31).transpose(1, 0, 2)),
        "bdw_col": _col(b_dw[0], 4), "gln": f(g_conv_ln[0]), "bln": f(b_conv_ln[0]),
        "gcat_col": _col(np.concatenate([f(g_out_attn[0]), f(g_out_conv[0])]), 8),
        "w_out": f(w_out[0]), "w_ff1": f(w_ff1[0]), "w_ff2": f(w_ff2[0]),
        "cs_all": np.ascontiguousarray(cs.reshape(NT, 128, 16).transpose(1, 0, 2)),
    }
    in_maps = []
    for core in range(8):
        b, j = core // 4, core % 4
        tiles = [slot_tile(j, n) for n in range(NSLOT)]
        xo = np.concatenate([x[b, t * 128:(t + 1) * 128] for t in tiles], axis=0)
        xhal = np.zeros((NSLOT * 32, D), np.float32)
        hfl = np.zeros((128, NSLOT), np.float32)
        lim = np.zeros((128, NSLOT), np.float32)
        ovr = np.full((128, NSLOT), 1.0e30, np.float32)
        cso = np.zeros((128, NSLOT, 16), np.float32)
        for n, t in enumerate(tiles):
            if t > 0:
                xhal[n * 32:(n + 1) * 32] = x[b, t * 128 - 32:t * 128]
                hfl[:, n] = 1.0
            else:
                xhal[n * 32:(n + 1) * 32] = 1.0
            tq = t * 128 + np.arange(128)
            lim[:, n] = (tq // 64 + 1) * 64 - (slot_ngroups(n) - 1) * 512
            ovr[(tq // 64 + 1) * 64 <= 256, n] = -1.0e29
            cso[:, n, :] = cs[t * 128:(t + 1) * 128]
        m = dict(shared)
        m.update({"xb": x[b], "xo": np.ascontiguousarray(xo), "xhal": xhal, "ccol": _col(np.asarray(c, np.float32)[b], 8),
                  "cs_own": cso, "limrel": lim, "hflag": hfl, "tauovr": ovr})
        in_maps.append(m)
    return in_maps


def assemble(results):
    out = np.zeros((2, S_LEN, D), np.float32)
    for core in range(8):
        b, j = core // 4, core % 4
        yc = np.asarray(results[core]["y"], np.float32)
        for n in range(NSLOT):
            t = slot_tile(j, n)
            out[b, t * 128:(t + 1) * 128] = yc[n * 128:(n + 1) * 128]
    return out


def kernel(**inputs):
    nc = build_nc()
    in_maps = make_in_maps(**inputs)
    res = run_bass_kernel_spmd(nc, in_maps, core_ids=list(range(8)))
    return assemble(res.results)
```

```python
import numpy as np
from contextlib import ExitStack
import concourse.bass as bass
import concourse.mybir as mybir
from concourse.bass_utils import run_bass_kernel_spmd

F32 = mybir.dt.float32
BF16 = mybir.dt.bfloat16
U8 = mybir.dt.uint8
AF = mybir.ActivationFunctionType
OP = mybir.AluOpType
AX = mybir.AxisListType

S_LEN = 8192
D = 1024
NT = 64
NG = 16
NSLOT = 16
CQ, CK, CV, CIQ, CIK, CIW, CCA, CCG = 0, 512, 1024, 1536, 2048, 2112, 2120, 2632
NCOL = 3144
EPS = 1e-6
NBIS = 14
BIS_R = 8.0
NEG = -1.0e30


_SLOT_NG = [1, 3, 5, 7, 9, 11, 13, 15, 16, 14, 12, 10, 8, 6, 4, 2]


def _slot_pe(n):
    ng = _SLOT_NG[n]
    return (ng - 1, 0) if ng <= 8 else (16 - ng, 1)


def slot_tile(j, n):
    p, e = _slot_pe(n)
    return 4 * p + j if e == 0 else 63 - 4 * p - j


def slot_ngroups(n):
    p, e = _slot_pe(n)
    return p + 1 if e == 0 else 16 - p


class Buf:
    def __init__(self, name):
        self.name = name
        self.lw = None
        self.rd = []


class Sched:
    def __init__(self, nc, es, n_dma_sems=24):
        self.nc = nc
        self.eng = {"pe": nc.tensor, "act": nc.scalar, "dve": nc.vector, "pool": nc.gpsimd, "sp": nc.sync}
        self.sem = {}
        self.cnt = {}
        for e in ["pe", "act", "dve", "pool"]:
            self.sem[e] = es.enter_context(nc.semaphore("c_" + e))
            self.cnt[e] = 0
        self.dsems = []
        for i in range(n_dma_sems):
            k = "d%d" % i
            self.sem[k] = es.enter_context(nc.semaphore(k))
            self.cnt[k] = 0
            self.dsems.append(k)
        self.dnext = 0
        self.rec = None
        self.seen = {e: {} for e in self.eng}

    def _wait(self, e, deps):
        best = {}
        for d in deps:
            if d is None:
                continue
            k, v = d
            if v <= 0 or (k == e and e == "pe"):
                continue
            if best.get(k, 0) < v:
                best[k] = v
        for k, v in best.items():
            if self.seen[e].get(k, 0) < v:
                self.eng[e].wait_ge(self.sem[k], v)
                self.seen[e][k] = v

    def _deps(self, e, reads, writes):
        deps = []
        for b in reads:
            deps.append(b.lw)
        for b in writes:
            deps.append(b.lw)
            for r in b.rd:
                deps.append(r)
        return deps

    def _mark(self, tok, reads, writes):
        for b in writes:
            b.lw = tok
            b.rd = []
        for b in reads:
            b.rd.append(tok)

    def op(self, e, fn, reads=(), writes=()):
        if self.rec is not None:
            self.rec.append(("op", e, fn, tuple(reads), tuple(writes)))
            return None
        self._wait(e, self._deps(e, reads, writes))
        inst = fn(self.eng[e])
        self.cnt[e] += 1
        inst.then_inc(self.sem[e], 1)
        tok = (e, self.cnt[e])
        self._mark(tok, reads, writes)
        return tok

    def dma(self, out, in_, reads=(), writes=(), q="sp", **kw):
        if self.rec is not None:
            self.rec.append(("dma", out, in_, tuple(reads), tuple(writes), q, kw))
            return None
        k = self.dsems[self.dnext]
        self.dnext = (self.dnext + 1) % len(self.dsems)
        deps = self._deps(q, reads, writes)
        deps.append((k, self.cnt[k]))
        self._wait(q, deps)
        self.eng[q].dma_start(out=out, in_=in_, **kw).then_inc(self.sem[k], 16)
        self.cnt[k] += 16
        tok = (k, self.cnt[k])
        self._mark(tok, reads, writes)
        return tok

    def mark(self):
        self.rec.append(("mark",))

    def record(self, fn):
        self.rec = []
        fn()
        r, self.rec = self.rec, None
        segs = [[]]
        for it in r:
            if it[0] == "mark":
                segs.append([])
            else:
                segs[-1].append(it)
        return segs

    def emit(self, it):
        if it[0] == "op":
            self.op(it[1], it[2], it[3], it[4])
        else:
            self.dma(it[1], it[2], it[3], it[4], it[5], **it[6])

    def pipeline(self, tiles, lead=None, hold_drain=False):
        T = len(tiles)
        nst = max(len(t) for t in tiles)
        steps = []
        for s_ in range(T + nst - 1):
            steps.append([tiles[s_ - k][k] for k in range(nst) if 0 <= s_ - k < T and k < len(tiles[s_ - k])])
        if lead:
            for i_, l_ in enumerate(lead):
                if i_ < len(steps):
                    steps[i_] = list(l_) + steps[i_]
                else:
                    steps.append(list(l_))
        n_emit = T if hold_drain else len(steps)
        for act in steps[:n_emit]:
            keyed = []
            for a, seg in enumerate(act):
                for j, it in enumerate(seg):
                    keyed.append(((j + 0.5) / len(seg), a, j, it))
            keyed.sort(key=lambda x: (x[0], x[1], x[2]))
            for _, _, _, it in keyed:
                self.emit(it)
        return steps[n_emit:]

    def soft_barrier(self, bufs):
        toks = []
        for b in bufs:
            toks.append(b.lw)
            toks.extend(b.rd)
        for e in self.eng:
            self._wait(e, toks)

    def barrier(self):
        toks = [(k, v) for k, v in self.cnt.items()]
        for e in self.eng:
            self._wait(e, toks)


def build_nc(dbg=None):
    nc = bass.Bass("TRN2", target_bir_lowering=False)
    di = lambda name, shape, dt=F32: nc.dram_tensor(name, shape, dt, kind="ExternalInput").ap()
    xb = di("xb", [S_LEN, D])
    xo = di("xo", [NSLOT * 128, D])
    xhal = di("xhal", [NSLOT * 32, D])
    ccol = di("ccol", [128, 8])
    w_ada = di("w_ada", [D, 6 * D])
    bada_col = di("bada_col", [128, 48])
    bada_row = di("bada_row", [6 * D])
    g1col = di("g1col", [128, 8])
    g2col = di("g2col", [128, 8])
    w_in = di("w_in", [D, NCOL])
    gq = di("gq", [64])
    gk = di("gk", [64])
    wdwT = di("wdwT", [128, 4, 31])
    bdw_col = di("bdw_col", [128, 4])
    gln = di("gln", [512])
    bln = di("bln", [512])
    gcat_col = di("gcat_col", [128, 8])
    w_out = di("w_out", [D, D])
    w_ff1 = di("w_ff1", [D, 4 * D])
    w_ff2 = di("w_ff2", [4 * D, D])
    cs_all = di("cs_all", [128, NT, 16])
    cs_own = di("cs_own", [128, NSLOT, 16])
    limrel = di("limrel", [128, NSLOT])
    tauovr = di("tauovr", [128, NSLOT])
    hflag = di("hflag", [128, NSLOT])
    y = nc.dram_tensor("y", [NSLOT * 128, D], F32, kind="ExternalOutput").ap()

    scr = lambda name, shape, dt=BF16: nc.dram_tensor(name, shape, dt, kind="Internal").ap()
    KT_scr = scr("KT_scr", [NG, 128, 4, 512])
    V_scr = scr("V_scr", [NG, 128, 4, 520])
    qT_scr = scr("qT_scr", [NSLOT, 128, 4, 128])
    iqT_scr = scr("iqT_scr", [NSLOT, 128, 4, 128])
    mix_scr = scr("mix_scr", [NSLOT, 128, 8, 128])
    b_KTs = [Buf("KTs%d" % g) for g in range(NG)]
    b_Vs = [Buf("Vs%d" % g) for g in range(NG)]
    b_qTs = [Buf("qTs%d" % n) for n in range(NSLOT)]
    b_iqTs = [Buf("iqTs%d" % n) for n in range(NSLOT)]
    b_mixa = [Buf("mixa%d" % n) for n in range(NSLOT)]
    b_mixc = [Buf("mixc%d" % n) for n in range(NSLOT)]
    b_y = [Buf("y%d" % n) for n in range(NSLOT)]

    dbg_out = {}
    if dbg:
        for name, (shape, dt) in dbg.items():
            dbg_out[name] = nc.dram_tensor("dbg_" + name, shape, dt, kind="ExternalOutput").ap()
    b_dbg = Buf("dbg")

    with ExitStack() as es:
        S = Sched(nc, es)

        def sbt(st, name, shape, dt):
            return st.enter_context(nc.sbuf_tensor(name, shape, dt)), Buf(name)

        def dump(name, ap, buf):
            if name in dbg_out:
                S.dma(dbg_out[name], ap, reads=[buf], writes=[b_dbg])

        PB = []
        bPB = []
        for i in range(8):
            PB.append(es.enter_context(nc.psum_tensor("pb%d" % i, [128, 512], F32)))
            bPB.append(Buf("pb%d" % i))
        PBh = [p.bitcast(BF16) for p in PB]

        ident_f, b_identf = sbt(es, "ident_f", [128, 128], F32)
        ident_b, b_identb = sbt(es, "ident_b", [128, 128], BF16)
        modcol, b_modcol = sbt(es, "modcol", [128, 4, 8], F32)
        gt_row, b_gtrow = sbt(es, "gt_row", [128, 2, 1024], F32)
        gmod, b_gmod = sbt(es, "gmod", [128, 2, 8], F32)
        gcat, b_gcat = sbt(es, "gcat", [128, 8], F32)
        small, b_small = sbt(es, "small", [128, 64], F32)

        S.op("pool", lambda g: g.memset(ident_f[:], 0.0), writes=[b_identf])
        S.op("pool", lambda g: g.affine_select(out=ident_f[:], in_=ident_f[:], pattern=[[-1, 128]],
                                               compare_op=OP.not_equal, fill=1.0, base=0, channel_multiplier=1),
             reads=[b_identf], writes=[b_identf])
        S.op("dve", lambda v: v.tensor_copy(ident_b[:], ident_f[:]), reads=[b_identf], writes=[b_identb])
        S.dma(gcat[:], gcat_col, writes=[b_gcat])

        mhalf, b_mhalf = sbt(es, "mhalf", [128, 8], F32)
        S.op("pool", lambda g: g.memset(mhalf[:], -0.5), writes=[b_mhalf])

        def rstd_from_ssq(dst, b_dst, n_part, inv_n, nfree=1):
            w = dst.shape[-1]
            S.op("dve", lambda v: v.tensor_scalar(out=dst, in0=dst, scalar1=inv_n, scalar2=EPS,
                                                  op0=OP.mult, op1=OP.add), reads=[b_dst], writes=[b_dst])
            S.op("pool", lambda g: g.tensor_tensor(out=dst, in0=dst, in1=mhalf[0:n_part, 0:w], op=OP.pow),
                 reads=[b_dst, b_mhalf], writes=[b_dst])

        with ExitStack() as ea:
            IKT2, b_IKT2 = sbt(ea, "IKT2", [128, S_LEN], BF16)
            iw_all, b_iw = sbt(ea, "iw_all", [128, NSLOT, 8], F32)
            lim_sb, b_lim = sbt(ea, "lim_sb", [128, NSLOT], F32)
            hfl_sb, b_hfl = sbt(ea, "hfl_sb", [128, NSLOT], F32)
            ovr_sb, b_ovr = sbt(ea, "ovr_sb", [128, NSLOT], F32)
            p012 = ExitStack()
            Wp, b_Wp = sbt(p012, "Wp", [128, 8, NCOL], BF16)
            bias_row, b_biasrow = sbt(p012, "bias_row", [128, CCA], F32)
            bias_cc, b_biascc = sbt(p012, "bias_cc", [128, 8], F32)
            csA, b_csA = sbt(p012, "csA", [128, NT, 16], F32)
            csO, b_csO = sbt(p012, "csO", [128, NSLOT, 16], F32)
            gq_row, b_gq = sbt(p012, "gq_row", [128, 64], F32)
            gk_row, b_gk = sbt(p012, "gk_row", [128, 64], F32)
            S.dma(csA[:], cs_all, writes=[b_csA])
            S.dma(csO[:], cs_own, writes=[b_csO])
            S.dma(gq_row[:], gq.partition_broadcast(128), writes=[b_gq])
            S.dma(gk_row[:], gk.partition_broadcast(128), writes=[b_gk])
            S.dma(lim_sb[:], limrel, writes=[b_lim])
            S.dma(hfl_sb[:], hflag, writes=[b_hfl])
            S.dma(ovr_sb[:], tauovr, writes=[b_ovr])

            silu_c, b_silu = sbt(p012, "silu_c", [128, 8], F32)
            badac, b_badac = sbt(p012, "badac", [128, 48], F32)
            screp_box = {}
            g12, b_g12 = sbt(p012, "g12", [128, 2, 8], F32)
            colidx = {0: 0, 1: 1, 3: 2, 4: 3}

            c2, b_c2 = sbt(p012, "c2", [128, 8, 2], BF16)
            mtmp, b_mtmp = sbt(p012, "mtmp", [128, 8], F32)

            def mod_block(m, wa, b_wa, wab, b_wab, pcol, prow):
                S.dma(wa[:], w_ada[:, m * D:(m + 1) * D].rearrange("(kc p) n -> p kc n", p=128), writes=[b_wa])
                S.op("act", lambda a: a.activation(out=wab[:], in_=wa[:], func=AF.Copy), reads=[b_wa], writes=[b_wab])
                if m in colidx:
                    def mm(t):
                        last = None
                        for fc in range(8):
                            for kc in range(8):
                                last = t.matmul(PB[pcol][:, 2 * fc:2 * fc + 2], lhsT=wab[:, kc, fc * 128:(fc + 1) * 128],
                                                rhs=c2[:, kc, :], start=(kc == 0), stop=(kc == 7))
                        return last
                    S.op("pe", mm, reads=[b_wab, b_c2], writes=[bPB[pcol]])
                    S.op("dve", lambda v: v.tensor_reduce(out=mtmp[:, :], in_=PB[pcol][:, 0:16].rearrange("p (f e) -> p f e", e=2),
                                                          axis=AX.X, op=OP.add), reads=[bPB[pcol]], writes=[b_mtmp])
                    S.op("dve", lambda v: v.tensor_tensor(out=modcol[:, colidx[m], :], in0=mtmp[:, :],
                                                          in1=badac[:, m * 8:(m + 1) * 8], op=OP.add),
                         reads=[b_mtmp, b_badac], writes=[b_modcol])
                    if m in (1, 4):
                        i = 0 if m == 1 else 1
                        S.op("dve", lambda v: v.scalar_tensor_tensor(out=gmod[:, i, :], in0=modcol[:, colidx[m], :], scalar=1.0,
                                                                     in1=g12[:, i, :], op0=OP.add, op1=OP.mult),
                             reads=[b_modcol, b_g12], writes=[b_gmod])
                else:
                    gi = 0 if m == 2 else 1
                    for cg_ in range(2):
                        def half(cg):
                            pb_ = prow[cg]

                            sc_rep2, b_screp, badar, b_badar = screp_box["v"]

                            def mm(t):
                                last = None
                                for kc in range(8):
                                    for e in range(2):
                                        last = t.matmul(PB[pb_][:, :], lhsT=sc_rep2[:, kc, e, :], rhs=wab[:, kc, cg * 512:(cg + 1) * 512],
                                                        start=(kc == 0 and e == 0), stop=(kc == 7 and e == 1))
                                return last
                            S.op("pe", mm, reads=[b_wab, b_screp], writes=[bPB[pb_]])
                            S.op("dve", lambda v: v.tensor_tensor(out=gt_row[:, gi, cg * 512:(cg + 1) * 512], in0=PB[pb_][:, :],
                                                                  in1=badar[:, gi, cg * 512:(cg + 1) * 512], op=OP.add),
                                 reads=[bPB[pb_], b_badar], writes=[b_gtrow])
                        half(cg_)

            with ExitStack() as p0:
                c_sb, b_c = sbt(p0, "c_sb", [128, 8], F32)
                sh_rep, b_shrep = sbt(p0, "sh_rep", [128, 8, 128], F32)
                was = [sbt(p0, "wa%d" % i, [128, 8, 1024], F32) for i in range(2)]
                wab0, b_wab0 = sbt(p0, "wab0", [128, 8, 1024], BF16)
                stw = [sbt(p0, "stw%d" % i, [128, NCOL], F32) for i in range(2)]
                S.dma(c_sb[:], ccol, writes=[b_c])
                S.dma(badac[:], bada_col, writes=[b_badac])
                S.dma(g12[:, 0, :], g1col, writes=[b_g12])
                S.dma(g12[:, 1, :], g2col, writes=[b_g12])
                S.op("act", lambda a: a.activation(out=silu_c[:], in_=c_sb[:], func=AF.Silu),
                     reads=[b_c], writes=[b_silu])
                S.op("dve", lambda v: v.tensor_copy(c2[:, :, 0], silu_c[:, :]), reads=[b_silu], writes=[b_c2])
                S.op("dve", lambda v: v.tensor_tensor(out=c2[:, :, 1], in0=silu_c[:, :], in1=c2[:, :, 0], op=OP.subtract),
                     reads=[b_silu, b_c2], writes=[b_c2])
                for m in range(2):
                    mod_block(m, was[m][0], was[m][1], wab0, b_wab0, 5, (6, 7))
                for kc in range(8):
                    S.op("dve", lambda v: v.tensor_copy(sh_rep[:, kc, :], modcol[:, 0, kc:kc + 1].to_broadcast([128, 128])),
                         reads=[b_modcol], writes=[b_shrep])
                S.op("pool", lambda g: g.memset(bias_cc[:], 0.0), writes=[b_biascc])
                rgroups = [(0, 512), (512, 512), (1024, 512), (1536, 512), (2048, 72)]
                for kc in range(8):
                    st, b_st = stw[kc % 2]
                    S.dma(st[:], w_in[kc * 128:(kc + 1) * 128, :], writes=[b_st])
                    for gi, (c0, cn) in enumerate(rgroups):
                        S.op("pe", lambda t: t.matmul(PB[gi][:, 0:cn], lhsT=sh_rep[:, kc, :], rhs=st[:, c0:c0 + cn],
                                                      start=(kc == 0), stop=(kc == 7)),
                             reads=[b_st, b_shrep], writes=[bPB[gi]])

                    def mmc(t):
                        last = None
                        for j in range(8):
                            last = t.matmul(PB[5][:, j:j + 1], lhsT=st[:, CCA + j * 128:CCA + (j + 1) * 128],
                                            rhs=modcol[:, 0, kc:kc + 1], start=True, stop=True)
                        return last
                    S.op("pe", mmc, reads=[b_st, b_modcol], writes=[bPB[5]])
                    S.op("dve", lambda v: v.tensor_tensor(out=bias_cc[:], in0=PB[5][:, 0:8], in1=bias_cc[:], op=OP.add),
                         reads=[bPB[5], b_biascc], writes=[b_biascc])
                    S.op("act", lambda a: a.activation(out=Wp[:, kc, :], in_=st[:], func=AF.Copy, scale=gmod[:, 0, kc:kc + 1]),
                         reads=[b_st, b_gmod], writes=[b_Wp])
                for gi, (c0, cn) in enumerate(rgroups):
                    S.op("act", lambda a: a.activation(out=bias_row[:, c0:c0 + cn], in_=PB[gi][:, 0:cn], func=AF.Copy),
                         reads=[bPB[gi]], writes=[b_biasrow])
                dump("modcol", modcol[:], b_modcol)
                dump("gt_row", gt_row[:], b_gtrow)
                dump("bias_row", bias_row[:], b_biasrow)
                dump("bias_cc", bias_cc[:], b_biascc)
                S.barrier()

            def norm_transpose(x_ap, b_x, npart, xh_t, b_xh, junk_t, b_junk, ssq_ap, b_ssq, pbank, xT_ap, b_xT):
                S.op("act", lambda a: a.activation(out=xh_t[:npart, :], in_=x_ap, func=AF.Square, accum_out=ssq_ap),
                     reads=[b_x], writes=[b_xh, b_ssq])
                rstd_from_ssq(ssq_ap, b_ssq, npart, 1.0 / D)
                S.op("act", lambda a: a.activation(out=xh_t[:npart, :], in_=x_ap, func=AF.Copy, scale=ssq_ap),
                     reads=[b_x, b_ssq], writes=[b_xh])

                def tr(t):
                    last = None
                    for kc in range(8):
                        last = t.transpose(PBh[pbank][:, kc * npart:(kc + 1) * npart],
                                           xh_t[:npart, kc * 128:(kc + 1) * 128], ident_b[:npart, :npart])
                    return last
                S.op("pe", tr, reads=[b_xh, b_identb], writes=[bPB[pbank]])
                S.op("dve", lambda v: v.tensor_copy(xT_ap, PBh[pbank][:, 0:8 * npart].rearrange("p (k t) -> p k t", k=8)),
                     reads=[bPB[pbank]], writes=[b_xT])

            def headnorm_rope(src_f, b_src, g_row, b_g, cs_ap, b_cs, dst_b, b_dst, tmp, b_tmp, st8, b_st8, nheads, do_norm,
                              tmpr=None, b_tmpr=None, mark_mid=False):
                sv = src_f.rearrange("p (h d) -> p h d", h=nheads)
                dv = dst_b.rearrange("p (h d) -> p h d", h=nheads)
                if do_norm:
                    tv = tmp[:, 0:nheads * 64]
                    S.op("act", lambda a: a.activation(out=tv, in_=src_f, func=AF.Square), reads=[b_src], writes=[b_tmp])
                    S.op("dve", lambda v: v.tensor_reduce(out=st8[:, 0:nheads], in_=tv.rearrange("p (h d) -> p h d", h=nheads),
                                                          axis=AX.X, op=OP.add), reads=[b_tmp], writes=[b_st8])
                    rstd_from_ssq(st8[:, 0:nheads], b_st8, 128, 1.0 / 64)
                    S.op("dve", lambda g: g.tensor_tensor(out=sv, in0=sv,
                                                          in1=st8[:, 0:nheads].unsqueeze(2).to_broadcast([128, nheads, 64]),
                                                          op=OP.mult), reads=[b_src, b_st8], writes=[b_src])
                    S.op("dve", lambda g: g.tensor_tensor(out=sv, in0=sv,
                                                          in1=g_row[:, :].unsqueeze(1).to_broadcast([128, nheads, 64]),
                                                          op=OP.mult), reads=[b_src, b_g], writes=[b_src])
                if mark_mid:
                    S.mark()
                S.op("act", lambda a: a.activation(out=dst_b, in_=src_f, func=AF.Copy), reads=[b_src], writes=[b_dst])
                cosb = cs_ap[:, 0:8].unsqueeze(1).to_broadcast([128, nheads, 8])
                sinb = cs_ap[:, 8:16].unsqueeze(1).to_broadcast([128, nheads, 8])
                if tmpr is None:
                    t1 = tmp[:, 512:512 + nheads * 8].rearrange("p (h d) -> p h d", h=nheads)
                    t2 = tmp[:, 576:576 + nheads * 8].rearrange("p (h d) -> p h d", h=nheads)
                else:
                    t1 = tmpr[:, 0:nheads * 8].rearrange("p (h d) -> p h d", h=nheads)
                    t2 = tmpr[:, 64:64 + nheads * 8].rearrange("p (h d) -> p h d", h=nheads)
                    b_tmp = b_tmpr
                x1 = sv[:, :, 0:8]
                x2 = sv[:, :, 8:16]
                S.op("pool", lambda g: g.tensor_tensor(out=t1, in0=x1, in1=cosb, op=OP.mult), reads=[b_src, b_cs], writes=[b_tmp])
                S.op("pool", lambda g: g.tensor_tensor(out=t2, in0=x2, in1=sinb, op=OP.mult), reads=[b_src, b_cs], writes=[b_tmp])
                S.op("pool", lambda g: g.tensor_tensor(out=dv[:, :, 0:8], in0=t1, in1=t2, op=OP.subtract),
                     reads=[b_tmp], writes=[b_dst])
                S.op("pool", lambda g: g.tensor_tensor(out=t1, in0=x2, in1=cosb, op=OP.mult), reads=[b_src, b_cs], writes=[b_tmp])
                S.op("pool", lambda g: g.tensor_tensor(out=t2, in0=x1, in1=sinb, op=OP.mult), reads=[b_src, b_cs], writes=[b_tmp])
                S.op("pool", lambda g: g.tensor_tensor(out=dv[:, :, 8:16], in0=t1, in1=t2, op=OP.add),
                     reads=[b_tmp], writes=[b_dst])

            def proj_tok(pbank, xT_ap, b_xT, c0, cn):
                def mm(t):
                    last = None
                    for kc in range(8):
                        last = t.matmul(PB[pbank][:, 0:cn], lhsT=xT_ap[:, kc, :], rhs=Wp[:, kc, c0:c0 + cn],
                                        start=(kc == 0), stop=(kc == 7))
                    return last
                S.op("pe", mm, reads=[b_xT, b_Wp], writes=[bPB[pbank]])

            with ExitStack() as p12:
                xt = [sbt(p12, "xt%d" % i, [128, D], F32) for i in range(2)]
                xh, b_xh = sbt(p12, "xh", [128, D], BF16)
                junk, b_junk = xh, b_xh
                ssq, b_ssq = sbt(p12, "ssq", [128, 2], F32)
                xT = [sbt(p12, "xT%d" % i, [128, 8, 160], BF16) for i in range(4)]
                kf = [sbt(p12, "kf%d" % i, [128, 512], F32) for i in range(3)]
                ikf = [sbt(p12, "ikf%d" % i, [128, 64], F32) for i in range(3)]
                tmp, b_tmp = sbt(p12, "tmp", [128, 640], F32)
                tmpB, b_tmpB = sbt(p12, "tmpB", [128, 128], F32)
                tmpC, b_tmpC = sbt(p12, "tmpC", [128, 128], F32)
                st8, b_st8 = sbt(p12, "st8", [128, 8], F32)
                kb = [sbt(p12, "kb%d" % i, [128, 512], BF16) for i in range(2)]
                ikb = [sbt(p12, "ikb%d" % i, [128, 128], BF16) for i in range(2)]
                KTg = [sbt(p12, "KTg%d" % i, [128, 4, 512], BF16) for i in range(2)]
                Vg = [sbt(p12, "Vg%d" % i, [128, 4, 520], BF16) for i in range(2)]
                for i in range(2):
                    S.op("pool", lambda g: g.memset(Vg[i][0][:], 1.0), writes=[Vg[i][1]])

                def p1_tile(i):
                    g, kt = i // 4, i % 4
                    ktg, b_ktg = KTg[g % 2]
                    vg, b_vg = Vg[g % 2]
                    x_t, b_x = xt[i % 2]
                    xT_t, b_xT = xT[i % 4]
                    kf_t, b_kf = kf[i % 3]
                    ikf_t, b_ikf = ikf[i % 3]
                    kb_t, b_kb = kb[i % 2]
                    ikb_t, b_ikb = ikb[i % 2]
                    S.dma(x_t[:], xb[i * 128:(i + 1) * 128, :], writes=[b_x])
                    norm_transpose(x_t[:], b_x, 128, xh, b_xh, junk, b_junk, ssq[:, 0:1], b_ssq, i % 2,
                                   xT_t[:, :, 0:128], b_xT)
                    S.mark()
                    proj_tok(2, xT_t[:, :, 0:128], b_xT, CK, 512)
                    S.op("dve", lambda v: v.tensor_tensor(out=kf_t[:], in0=PB[2][:, :], in1=bias_row[:, CK:CK + 512], op=OP.add),
                         reads=[bPB[2], b_biasrow], writes=[b_kf])
                    proj_tok(3, xT_t[:, :, 0:128], b_xT, CV, 512)
                    S.op("dve", lambda v: v.tensor_tensor(
                        out=vg[:, kt, :].rearrange("p (h d) -> p h d", h=8)[:, :, 0:64],
                        in0=PB[3][:, :].rearrange("p (h d) -> p h d", h=8),
                        in1=bias_row[:, CV:CV + 512].rearrange("p (h d) -> p h d", h=8), op=OP.add),
                         reads=[bPB[3], b_biasrow], writes=[b_vg])
                    proj_tok(4, xT_t[:, :, 0:128], b_xT, CIK, 64)
                    S.op("dve", lambda v: v.tensor_tensor(out=ikf_t[:], in0=PB[4][:, 0:64], in1=bias_row[:, CIK:CIK + 64], op=OP.add),
                         reads=[bPB[4], b_biasrow], writes=[b_ikf])
                    if kt == 3:
                        S.dma(V_scr[g], vg[:], reads=[b_vg], writes=[b_Vs[g]])
                    S.mark()
                    headnorm_rope(kf_t[:, :], b_kf, gk_row, b_gk, csA[:, i, :], b_csA, kb_t[:, :], b_kb, tmp, b_tmp, st8, b_st8, 8, True,
                                  tmpr=tmpB, b_tmpr=b_tmpB, mark_mid=True)
                    headnorm_rope(ikf_t[:, :], b_ikf, None, None, csA[:, i, :], b_csA, ikb_t[:, 0:64], b_ikb, tmp, b_tmp, st8, b_st8, 1, False,
                                  tmpr=tmpB, b_tmpr=b_tmpB)
                    S.op("act", lambda a: a.activation(out=ikb_t[:, 64:128], in_=ikb_t[:, 0:64], func=AF.Copy), reads=[b_ikb], writes=[b_ikb])
                    S.mark()
                    pb2 = 5 + (i % 2)

                    def tr2(t):
                        for pr in range(4):
                            t.transpose(PBh[pb2][:, pr * 128:(pr + 1) * 128], kb_t[:, pr * 128:(pr + 1) * 128], ident_b[:])
                        return t.transpose(PBh[pb2][:, 512:640], ikb_t[:, :], ident_b[:])
                    S.op("pe", tr2, reads=[b_kb, b_ikb, b_identb], writes=[bPB[pb2]])
                    S.op("act", lambda a: a.activation(out=ktg[:, :, kt * 128:(kt + 1) * 128],
                                                       in_=PBh[pb2][:, 0:512].rearrange("p (k t) -> p k t", k=4), func=AF.Copy),
                         reads=[bPB[pb2]], writes=[b_ktg])
                    S.op("act", lambda a: a.activation(out=IKT2[:, i * 128:(i + 1) * 128], in_=PBh[pb2][:, 512:640], func=AF.Copy),
                         reads=[bPB[pb2]], writes=[b_IKT2])
                    if kt == 3:
                        S.dma(KT_scr[g], ktg[:], reads=[b_ktg], writes=[b_KTs[g]])

                with ExitStack() as pw:
                    wa1, b_wa1 = sbt(pw, "wadefer", [128, 8, 1024], F32)
                    wab1, b_wab1 = sbt(pw, "wab1", [128, 8, 1024], BF16)
                    sc_rep, b_screp = sbt(pw, "sc_rep2", [128, 8, 2, 128], BF16)
                    badar, b_badar = sbt(pw, "badar", [128, 2, 1024], F32)
                    screp_box["v"] = (sc_rep, b_screp, badar, b_badar)
                    S.dma(badar[:, 0, :], bada_row[2 * D:3 * D].partition_broadcast(128), writes=[b_badar])
                    S.dma(badar[:, 1, :], bada_row[5 * D:6 * D].partition_broadcast(128), reads=[], writes=[b_badar])
                    for kc in range(8):
                        S.op("dve", lambda v: v.tensor_copy(sc_rep[:, kc, :, :], c2[:, kc, :].unsqueeze(2).to_broadcast([128, 2, 128])),
                             reads=[b_c2], writes=[b_screp])
                    tiles1 = [S.record(lambda: p1_tile(i)) for i in range(NT)]
                    for i_, m_ in enumerate((2, 3, 4, 5)):
                        tiles1[4 * i_].append(S.record(lambda: mod_block(m_, wa1, b_wa1, wab1, b_wab1, 7, (7, 7)))[0])
                    p1_drain = S.pipeline(tiles1, hold_drain=True)
                    S.soft_barrier([b_wa1, b_wab1, b_screp, b_badar])
                dump("IKT2", IKT2[:], b_IKT2)

                with ExitStack() as p2:
                    dgc, b_dgc = sbt(p2, "dgc", [128, 4, 31, 128], BF16)
                    b_dgc2 = Buf("dgc2")
                    wdw_sb, b_wdw = sbt(p2, "wdw_sb", [128, 4, 31], F32)
                    bdw_sb, b_bdw = sbt(p2, "bdw_sb", [128, 4], F32)
                    gln_sb, b_gln = sbt(p2, "gln_sb", [128, 512], F32)
                    bln_sb, b_bln = sbt(p2, "bln_sb", [128, 512], F32)
                    xhl, b_xhl = sbt(p2, "xhl", [32, D], F32)
                    qf, b_qf = sbt(p2, "qf", [128, 512], F32)
                    qb, b_qb = sbt(p2, "qb", [128, 512], BF16)
                    qf2, b_qf2 = sbt(p2, "qf2", [128, 512], F32)
                    tmpq, b_tmpq = sbt(p2, "tmpq", [128, 512], F32)
                    st8q, b_st8q = sbt(p2, "st8q", [128, 8], F32)
                    tmpBq, b_tmpBq = sbt(p2, "tmpBq", [128, 128], F32)
                    qb2, b_qb2 = sbt(p2, "qb2", [128, 512], BF16)
                    qT, b_qT = sbt(p2, "qT", [128, 4, 128], BF16)
                    iqT, b_iqT = sbt(p2, "iqT", [128, 4, 128], BF16)
                    sgs = [sbt(p2, "sg%d" % i, [128, 160], F32) for i in range(2)]
                    uTs = [sbt(p2, "uT%d" % i, [128, 4, 160], BF16) for i in range(2)]
                    yT, b_yT = sbt(p2, "yT", [128, 4, 128], F32)
                    ysb = [sbt(p2, "ysb%d" % i, [128, 512], F32) for i in range(2)]
                    yn, b_yn = sbt(p2, "yn", [128, 512], F32)
                    bst, b_bst = sbt(p2, "bst", [128, 8], F32)
                    junkD, b_junkD = sbt(p2, "junkD", [128, 512], BF16)
                    sgD, b_sgD = sbt(p2, "sgD", [128, 512], F32)
                    stD, b_stD = sbt(p2, "stD", [128, 1], F32)
                    cvb, b_cvb = sbt(p2, "cvb", [128, 512], BF16)
                    mixc, b_mixcs = sbt(p2, "mixc", [128, 4, 128], BF16)
                    S.dma(wdw_sb[:], wdwT, writes=[b_wdw])
                    S.dma(bdw_sb[:], bdw_col, writes=[b_bdw])
                    S.dma(gln_sb[:], gln.partition_broadcast(128), writes=[b_gln])
                    S.dma(bln_sb[:], bln.partition_broadcast(128), writes=[b_bln])
                    def build_dgc():
                        def one(cc, j):
                            if (cc * 31 + j) % 3 == 0:
                                S.op("dve", lambda g: g.tensor_scalar(out=dgc[:, cc, j, :], in0=ident_f[:], scalar1=wdw_sb[:, cc, j:j + 1],
                                                                      scalar2=None, op0=OP.mult),
                                     reads=[b_identf, b_wdw], writes=[b_dgc])
                            else:
                                S.op("act", lambda a: a.activation(out=dgc[:, cc, j, :], in_=ident_f[:], func=AF.Copy,
                                                                   scale=wdw_sb[:, cc, j:j + 1]),
                                     reads=[b_identf, b_wdw], writes=[b_dgc2])
                        for cc in range(4):
                            for j in range(31):
                                one(cc, j)

                    def p2_slot(n):
                        x_t, b_x = xt[n % 2]
                        xT_t, b_xT = xT[n % 4]
                        y_t, b_ys = ysb[n % 2]
                        uT, b_uT = uTs[n % 2]
                        S.dma(x_t[:], xo[n * 128:(n + 1) * 128, :], writes=[b_x])
                        S.dma(xhl[:], xhal[n * 32:(n + 1) * 32, :], writes=[b_xhl])
                        norm_transpose(x_t[:], b_x, 128, xh, b_xh, junk, b_junk, ssq[:, 0:1], b_ssq, 0, xT_t[:, :, 32:160], b_xT)
                        norm_transpose(xhl[:], b_xhl, 32, xh, b_xh, junk, b_junk, ssq[0:32, 1:2], b_ssq, 0, xT_t[:, :, 0:32], b_xT)
                        S.mark()
                        own = xT_t[:, :, 32:160]
                        proj_tok(1, own, b_xT, CQ, 512)
                        S.op("dve", lambda v: v.tensor_tensor(out=qf[:], in0=PB[1][:, :], in1=bias_row[:, CQ:CQ + 512], op=OP.add),
                             reads=[bPB[1], b_biasrow], writes=[b_qf])
                        headnorm_rope(qf[:, :], b_qf, gq_row, b_gq, csO[:, n, :], b_csO, qb[:, :], b_qb, tmpq, b_tmpq, st8q, b_st8q, 8, True,
                                      tmpr=tmpBq, b_tmpr=b_tmpBq)

                        def trq(t):
                            last = None
                            for pr in range(4):
                                last = t.transpose(PBh[7][:, pr * 128:(pr + 1) * 128], qb[:, pr * 128:(pr + 1) * 128], ident_b[:])
                            return last
                        S.op("pe", trq, reads=[b_qb, b_identb], writes=[bPB[7]])
                        S.op("act", lambda a: a.activation(out=qT[:], in_=PBh[7][:, 0:512].rearrange("p (k t) -> p k t", k=4), func=AF.Copy),
                             reads=[bPB[7]], writes=[b_qT])
                        S.dma(qT_scr[n], qT[:], reads=[b_qT], writes=[b_qTs[n]])
                        S.mark()
                        proj_tok(3, own, b_xT, CIQ, 512)
                        S.op("dve", lambda v: v.tensor_tensor(out=qf2[:], in0=PB[3][:, :], in1=bias_row[:, CIQ:CIQ + 512], op=OP.add),
                             reads=[bPB[3], b_biasrow], writes=[b_qf2])
                        headnorm_rope(qf2[:, :], b_qf2, None, None, csO[:, n, :], b_csO, qb2[:, :], b_qb2, tmp, b_tmp, st8, b_st8, 8, False,
                                      tmpr=tmpC, b_tmpr=b_tmpC)

                        def triq(t):
                            last = None
                            for pr in range(4):
                                last = t.transpose(PBh[7][:, 512 + pr * 128:512 + (pr + 1) * 128], qb2[:, pr * 128:(pr + 1) * 128], ident_b[:])
                            return last
                        S.op("pe", triq, reads=[b_qb2, b_identb], writes=[bPB[7]])
                        S.op("act", lambda a: a.activation(out=iqT[:], in_=PBh[7][:, 512:1024].rearrange("p (k t) -> p k t", k=4), func=AF.Copy),
                             reads=[bPB[7]], writes=[b_iqT])
                        S.dma(iqT_scr[n], iqT[:], reads=[b_iqT], writes=[b_iqTs[n]])
                        proj_tok(3, own, b_xT, CIK, 72)
                        S.op("dve", lambda v: v.tensor_tensor(out=iw_all[:, n, :], in0=PB[3][:, 64:72], in1=bias_row[:, CIW:CIW + 8], op=OP.add),
                             reads=[bPB[3], b_biasrow], writes=[b_iw])
                        S.op("dve", lambda v: v.tensor_scalar(out=iw_all[:, n, :], in0=iw_all[:, n, :], scalar1=float(8 ** -0.5 * 64 ** -0.5),
                                                              scalar2=None, op0=OP.mult), reads=[b_iw], writes=[b_iw])
                        S.mark()
                        for cc_ in range(4):
                            def conv_in(cc):
                                pbi = 4
                                sg_t, b_sg_t = sgs[cc % 2]

                                def mmc(t):
                                    last = None
                                    for half, c0 in ((0, CCA), (1, CCG)):
                                        for kc in range(8):
                                            last = t.matmul(PB[pbi][:, half * 160:(half + 1) * 160],
                                                            lhsT=Wp[:, kc, c0 + cc * 128:c0 + (cc + 1) * 128], rhs=xT_t[:, kc, :],
                                                            start=(kc == 0), stop=(kc == 7))
                                    return last
                                S.op("pe", mmc, reads=[b_xT, b_Wp], writes=[bPB[pbi]])
                                S.op("act", lambda a: a.activation(out=sg_t[:], in_=PB[pbi][:, 160:320], func=AF.Sigmoid,
                                                                   bias=bias_cc[:, 4 + cc:5 + cc]),
                                     reads=[bPB[pbi], b_biascc], writes=[b_sg_t])
                                S.op("dve", lambda v: v.scalar_tensor_tensor(out=uT[:, cc, :], in0=PB[pbi][:, 0:160], scalar=bias_cc[:, cc:cc + 1],
                                                                             in1=sg_t[:], op0=OP.add, op1=OP.mult),
                                     reads=[bPB[pbi], b_biascc, b_sg_t], writes=[b_uT])
                            conv_in(cc_)
                        S.op("dve", lambda v: v.tensor_scalar(out=uT[:, :, 0:32], in0=uT[:, :, 0:32], scalar1=hfl_sb[:, n:n + 1],
                                                              scalar2=None, op0=OP.mult), reads=[b_uT, b_hfl], writes=[b_uT])
                        S.mark()
                        for cc_ in range(4):
                            def conv_dw(cc):
                                pbo = (PB[5], bPB[5])

                                def mcv(t):
                                    last = None
                                    for j in range(31):
                                        last = t.matmul(pbo[0][:, 0:128], lhsT=dgc[:, cc, j, :], rhs=uT[:, cc, 2 + j:2 + j + 128],
                                                        start=(j == 0), stop=(j == 30))
                                    return last
                                S.op("pe", mcv, reads=[b_uT, b_dgc, b_dgc2], writes=[pbo[1]])
                                S.op("act", lambda a: a.activation(out=yT[:, cc, :], in_=pbo[0][:, 0:128], func=AF.Identity,
                                                                   bias=bdw_sb[:, cc:cc + 1]),
                                     reads=[pbo[1], b_bdw], writes=[b_yT])
                            conv_dw(cc_)

                        def try_(t):
                            last = None
                            for cc in range(4):
                                last = t.transpose(PB[6][:, cc * 128:(cc + 1) * 128], yT[:, cc, :], ident_f[:])
                            return last
                        S.op("pe", try_, reads=[b_yT, b_identf], writes=[bPB[6]])
                        S.op("dve", lambda v: v.tensor_copy(y_t[:], PB[6][:, :]), reads=[bPB[6]], writes=[b_ys])
                        S.mark()
                        S.op("dve", lambda v: v.bn_stats(out=bst[:, 0:6], in_=y_t[:]), reads=[b_ys], writes=[b_bst])
                        S.op("dve", lambda v: v.bn_aggr(out=bst[:, 6:8], in_=bst[:, 0:6]), reads=[b_bst], writes=[b_bst])
                        S.op("dve", lambda v: v.tensor_scalar(out=bst[:, 7:8], in0=bst[:, 7:8], scalar1=EPS, scalar2=None, op0=OP.add),
                             reads=[b_bst], writes=[b_bst])
                        S.op("pool", lambda g: g.tensor_tensor(out=bst[:, 7:8], in0=bst[:, 7:8], in1=mhalf[:, 0:1], op=OP.pow),
                             reads=[b_bst, b_mhalf], writes=[b_bst])
                        S.op("dve", lambda v: v.tensor_scalar(out=yn[:], in0=y_t[:], scalar1=bst[:, 6:7], scalar2=bst[:, 7:8],
                                                              op0=OP.subtract, op1=OP.mult), reads=[b_ys, b_bst], writes=[b_yn])
                        S.op("pool", lambda g: g.tensor_tensor(out=yn[:], in0=yn[:], in1=gln_sb[:], op=OP.mult), reads=[b_yn, b_gln], writes=[b_yn])
                        S.op("pool", lambda g: g.tensor_tensor(out=yn[:], in0=yn[:], in1=bln_sb[:], op=OP.add), reads=[b_yn, b_bln], writes=[b_yn])
                        S.op("act", lambda a: a.activation(out=sgD[:], in_=yn[:], func=AF.Sigmoid), reads=[b_yn], writes=[b_sgD])
                        S.op("pool", lambda g: g.tensor_tensor(out=yn[:], in0=yn[:], in1=sgD[:], op=OP.mult), reads=[b_yn, b_sgD], writes=[b_yn])
                        S.op("act", lambda a: a.activation(out=junkD[:], in_=yn[:], func=AF.Square, accum_out=stD[:, 0:1]),
                             reads=[b_yn], writes=[b_junkD, b_stD])
                        rstd_from_ssq(stD[:, 0:1], b_stD, 128, 1.0 / 512)
                        S.op("act", lambda a: a.activation(out=cvb[:], in_=yn[:], func=AF.Copy, scale=stD[:, 0:1]),
                             reads=[b_yn, b_stD], writes=[b_cvb])

                        def trc(t):
                            last = None
                            for pr in range(4):
                                last = t.transpose(PBh[2][:, pr * 128:(pr + 1) * 128], cvb[:, pr * 128:(pr + 1) * 128], ident_b[:])
                            return last
                        S.op("pe", trc, reads=[b_cvb, b_identb], writes=[bPB[2]])
                        S.op("act", lambda a: a.activation(out=mixc[:], in_=PBh[2][:, 0:512].rearrange("p (k t) -> p k t", k=4), func=AF.Copy),
                             reads=[bPB[2]], writes=[b_mixcs])
                        S.dma(mix_scr[n][:, 4:8, :], mixc[:], reads=[b_mixcs], writes=[b_mixc[n]])

                    tiles2 = [S.record(lambda: p2_slot(n)) for n in range(NSLOT)]
                    dg_ops = S.record(build_dgc)[0]
                    for i_ in range(3):
                        part = dg_ops[i_::3]
                        base = tiles2[i_][0]
                        merged = []
                        for q_ in range(max(len(part), len(base))):
                            if q_ < len(base):
                                merged.append(base[q_])
                            merged.extend(part[4 * q_:4 * q_ + 4])
                        merged.extend(part[4 * max(len(part), len(base)):])
                        tiles2[i_][0] = merged
                    S.pipeline(tiles2, lead=p1_drain)
                    dump("iw_all", iw_all[:], b_iw)
                S.barrier()
            p012.close()

            with ExitStack() as p3:
                score = [sbt(p3, "score%d" % i, [128, S_LEN], F32) for i in range(2)]
                Mall = [sbt(p3, "Mall%d" % i, [128, S_LEN], BF16) for i in range(2)]
                junk8, b_junk8 = sbt(p3, "junk8", [128, S_LEN], U8)
                iota, b_iota = sbt(p3, "iota", [128, 512], F32)
                pen, b_pen = sbt(p3, "pen", [128, 512], F32)
                qT3 = [sbt(p3, "qT3_%d" % i, [128, 4, 128], BF16) for i in range(2)]
                iqT3 = [sbt(p3, "iqT3_%d" % i, [128, 4, 128], BF16) for i in range(2)]
                dgw, b_dgw = sbt(p3, "dgw", [128, 8, 128], BF16)
                rl = [sbt(p3, "rl%d" % i, [128, 512], BF16) for i in range(4)]
                bis = [sbt(p3, "bis%d" % i, [128, 12], F32) for i in range(2)]
                mk, b_mk = sbt(p3, "mk", [128, 4096], F32)
                t16, b_t16 = sbt(p3, "t16", [128, 40], F32)
                iota8, b_iota8 = sbt(p3, "iota8", [128, 8], F32)
                R4, b_R4 = sbt(p3, "R4", [128, 512], BF16)
                KT3 = [sbt(p3, "KT3_%d" % i, [128, 4, 512], BF16) for i in range(2)]
                V3 = [sbt(p3, "V3_%d" % i, [128, 4, 520], BF16) for i in range(2)]
                Ex = [sbt(p3, "Ex%d" % i, [128, 512], BF16) for i in range(4)]
                LB = [0, 1, 4, 5]
                SKEW = 3
                rden, b_rden = sbt(p3, "rden", [128, 8], F32)
                attn, b_attn = sbt(p3, "attn", [128, 512], F32)
                ajunk, b_ajunk = sbt(p3, "ajunk", [128, 512], BF16)
                ast, b_ast = sbt(p3, "ast", [128, 1], F32)
                ab, b_ab = sbt(p3, "ab", [128, 512], BF16)
                mixa, b_mixas = sbt(p3, "mixa", [128, 4, 128], BF16)
                S.op("pool", lambda g: g.iota(iota[:], pattern=[[1, 512]], base=0, channel_multiplier=0,
                                              allow_small_or_imprecise_dtypes=True), writes=[b_iota])
                S.op("pool", lambda g: g.iota(iota8[:], pattern=[[1, 8]], base=0, channel_multiplier=0,
                                              allow_small_or_imprecise_dtypes=True), writes=[b_iota8])
                for r_ in range(4):
                    S.op("pool", lambda g: g.tensor_copy(R4[:, r_ * 128:(r_ + 1) * 128], ident_b[:]), reads=[b_identb], writes=[b_R4])

                glist = [(n, g) for n in range(NSLOT) for g in range(slot_ngroups(n))]
                gpos = {ng_: k for k, ng_ in enumerate(glist)}

                def kv_dma(k):
                    if k >= len(glist):
                        return
                    n_, g_ = glist[k]
                    S.dma(KT3[k % 2][0][:], KT_scr[g_], reads=[b_KTs[g_]], writes=[KT3[k % 2][1]])
                    S.dma(V3[k % 2][0][:], V_scr[g_], reads=[b_Vs[g_]], writes=[V3[k % 2][1]])

                def prep(n):
                    q_t, b_q = qT3[n % 2]
                    iq_t, b_iq = iqT3[n % 2]
                    S.dma(iq_t[:], iqT_scr[n], reads=[b_iqTs[n]], writes=[b_iq])
                    S.dma(q_t[:], qT_scr[n], reads=[b_qTs[n]], writes=[b_q])
                    for h in range(8):
                        S.op("dve", lambda v: v.tensor_scalar(out=dgw[:, h, :], in0=ident_f[:], scalar1=iw_all[:, n, h:h + 1],
                                                              scalar2=None, op0=OP.mult),
                             reads=[b_identf, b_iw], writes=[b_dgw])
                    S.op("dve", lambda v: v.tensor_scalar(out=pen[:], in0=iota[:], scalar1=lim_sb[:, n:n + 1], scalar2=NEG,
                                                          op0=OP.is_ge, op1=OP.mult), reads=[b_iota, b_lim], writes=[b_pen])

                def idx(n):
                    ng = slot_ngroups(n)
                    iq_t, b_iq = iqT3[n % 2]
                    sc_t, b_sc = score[n % 2]
                    pitems = [(g, j) for g in range(ng) for j in range(4)]

                    def zpair(pi):
                        g, j = pitems[pi]
                        b0, b1 = LB[2 * (pi % 2)], LB[2 * (pi % 2) + 1]

                        def f(t):
                            t.matmul(PB[b0][:, :], lhsT=iq_t[0:64, j, :], rhs=IKT2[0:64, g * 512:(g + 1) * 512], start=True, stop=True)
                            return t.matmul(PB[b1][:, :], lhsT=iq_t[64:128, j, :], rhs=IKT2[64:128, g * 512:(g + 1) * 512],
                                            start=True, stop=True)
                        S.op("pe", f, reads=[b_iq, b_IKT2], writes=[bPB[b0], bPB[b1]])
                    zpair(0)
                    for pi, (g, j) in enumerate(pitems):
                        if pi + 1 < len(pitems):
                            zpair(pi + 1)
                        sb_ = 2 + (g % 2)
                        for e in range(2):
                            h = 2 * j + e
                            zb = LB[2 * (pi % 2) + e]
                            r_t, b_r = rl[(2 * pi + e) % 4]
                            S.op("act", lambda a: a.activation(out=r_t[:], in_=PB[zb][:, :], func=AF.Relu),
                                 reads=[bPB[zb]], writes=[b_r])
                            S.op("pe", lambda t: t.matmul(PB[sb_][:, :], lhsT=dgw[:, h, :], rhs=r_t[:], start=(h == 0), stop=(h == 7)),
                                 reads=[b_dgw, b_r], writes=[bPB[sb_]])
                        if j == 3:
                            if g == ng - 1:
                                S.op("dve", lambda v: v.tensor_tensor(out=sc_t[:, g * 512:(g + 1) * 512], in0=PB[sb_][:, :], in1=pen[:], op=OP.add),
                                     reads=[bPB[sb_], b_pen], writes=[b_sc])
                            else:
                                S.op("act", lambda a: a.activation(out=sc_t[:, g * 512:(g + 1) * 512], in_=PB[sb_][:, :], func=AF.Copy),
                                     reads=[bPB[sb_]], writes=[b_sc])

                def bisect(n):
                    nk = slot_ngroups(n) * 512
                    sc_t, b_sc = score[n % 2]
                    bs, b_bs = bis[n % 2]
                    S.op("dve", lambda v: v.memset(bs[:, 0:1], 0.0), writes=[b_bs])
                    for k in range(NBIS):
                        S.op("dve", lambda v: v.tensor_scalar(out=junk8[:, 0:nk], in0=sc_t[:, 0:nk], scalar1=bs[:, 0:1], scalar2=0.0,
                                                              op0=OP.is_ge, op1=OP.add, accum_out=bs[:, 1:2]),
                             reads=[b_sc, b_bs], writes=[b_junk8, b_bs])
                        wk = BIS_R / (2 ** k)
                        if k < NBIS - 1:
                            wn = wk / 2
                            S.op("dve", lambda v: v.tensor_scalar(out=bs[:, 2:3], in0=bs[:, 1:2], scalar1=255.5, scalar2=2 * wn,
                                                                  op0=OP.is_gt, op1=OP.mult), reads=[b_bs], writes=[b_bs])
                            S.op("dve", lambda v: v.scalar_tensor_tensor(out=bs[:, 0:1], in0=bs[:, 2:3], scalar=-wn, in1=bs[:, 0:1],
                                                                         op0=OP.add, op1=OP.add), reads=[b_bs], writes=[b_bs])
                        else:
                            S.op("dve", lambda v: v.tensor_scalar(out=bs[:, 2:3], in0=bs[:, 1:2], scalar1=255.5, scalar2=wk,
                                                                  op0=OP.is_gt, op1=OP.mult), reads=[b_bs], writes=[b_bs])
                            S.op("dve", lambda v: v.scalar_tensor_tensor(out=bs[:, 4:5], in0=bs[:, 2:3], scalar=-wk, in1=bs[:, 0:1],
                                                                         op0=OP.add, op1=OP.add), reads=[b_bs], writes=[b_bs])
                            S.op("dve", lambda v: v.tensor_scalar(out=bs[:, 5:6], in0=bs[:, 4:5], scalar1=wk, scalar2=None, op0=OP.add),
                                 reads=[b_bs], writes=[b_bs])
                    S.op("dve", lambda v: v.tensor_scalar(out=junk8[:, 0:nk], in0=sc_t[:, 0:nk], scalar1=bs[:, 5:6], scalar2=0.0,
                                                          op0=OP.is_ge, op1=OP.add, accum_out=bs[:, 6:7]),
                         reads=[b_sc, b_bs], writes=[b_junk8, b_bs])
                    S.op("dve", lambda v: v.tensor_scalar(out=bs[:, 7:8], in0=bs[:, 6:7], scalar1=-1.0, scalar2=255.0,
                                                          op0=OP.mult, op1=OP.add), reads=[b_bs], writes=[b_bs])
                    npc = (nk + 4095) // 4096
                    for pc in range(npc):
                        c0, c1 = pc * 4096, min(nk, (pc + 1) * 4096)
                        S.op("dve", lambda v: v.scalar_tensor_tensor(out=mk[:, 0:c1 - c0], in0=junk8[:, c0:c1], scalar=NEG, in1=sc_t[:, c0:c1],
                                                                     op0=OP.mult, op1=OP.add), reads=[b_junk8, b_sc], writes=[b_mk])
                        S.op("dve", lambda v: v.max(out=t16[:, pc * 8:(pc + 1) * 8], in_=mk[:, 0:c1 - c0]), reads=[b_mk], writes=[b_t16])
                    if npc == 2:
                        S.op("dve", lambda v: v.max(out=t16[:, 16:24], in_=t16[:, 0:16]), reads=[b_t16], writes=[b_t16])
                        t8 = t16[:, 16:24]
                    else:
                        t8 = t16[:, 0:8]
                    S.op("dve", lambda v: v.tensor_scalar(out=t16[:, 24:32], in0=t8, scalar1=bs[:, 4:5], scalar2=None, op0=OP.subtract),
                         reads=[b_t16, b_bs], writes=[b_t16])
                    S.op("dve", lambda v: v.scalar_tensor_tensor(out=t16[:, 32:40], in0=iota8[:], scalar=bs[:, 7:8], in1=t16[:, 24:32],
                                                                 op0=OP.is_equal, op1=OP.mult, accum_out=bs[:, 8:9]),
                         reads=[b_iota8, b_t16, b_bs], writes=[b_t16, b_bs])
                    S.op("dve", lambda v: v.tensor_tensor(out=bs[:, 3:4], in0=bs[:, 8:9], in1=bs[:, 4:5], op=OP.add),
                         reads=[b_bs], writes=[b_bs])
                    S.op("dve", lambda v: v.tensor_tensor(out=bs[:, 3:4], in0=bs[:, 3:4], in1=ovr_sb[:, n:n + 1], op=OP.min),
                         reads=[b_bs, b_ovr], writes=[b_bs])
                    m_t, b_m = Mall[n % 2]
                    S.op("dve", lambda v: v.tensor_scalar(out=m_t[:, 0:nk], in0=sc_t[:, 0:nk], scalar1=bs[:, 3:4],
                                                          scalar2=-30000.0, op0=OP.is_lt, op1=OP.mult),
                         reads=[b_sc, b_bs], writes=[b_m])
                    if n == 1:
                        dump("score1", sc_t[:], b_sc)
                        dump("bis1", bs[:], b_bs)

                def att_main(n):
                    ng = slot_ngroups(n)
                    q_t, b_q = qT3[n % 2]
                    m_t, b_m = Mall[n % 2]
                    units = [(g, kt) for g in range(ng) for kt in range(4)]

                    def lunit(ui):
                        g, kt = units[ui]
                        k = gpos[(n, g)]
                        kt_t, b_kt = KT3[k % 2]
                        bA, bB = LB[2 * (ui % 2)], LB[2 * (ui % 2) + 1]
                        c0 = g * 512 + kt * 128

                        def mml(t):
                            for j in range(4):
                                t.matmul(PB[bA][:, j * 128:(j + 1) * 128], lhsT=kt_t[0:64, j, kt * 128:(kt + 1) * 128],
                                         rhs=q_t[0:64, j, :], start=(j == 0), stop=False, skip_group_check=True)
                                t.matmul(PB[bB][:, j * 128:(j + 1) * 128], lhsT=kt_t[64:128, j, kt * 128:(kt + 1) * 128],
                                         rhs=q_t[64:128, j, :], start=(j == 0), stop=False, skip_group_check=True)
                            t.matmul(PB[bA][:, :], lhsT=m_t[:, c0:c0 + 128], rhs=R4[:], start=False, stop=True, skip_group_check=True)
                            return t.matmul(PB[bB][:, :], lhsT=m_t[:, c0:c0 + 128], rhs=R4[:], start=False, stop=True,
                                            skip_group_check=True)
                        S.op("pe", mml, reads=[b_kt, b_q, b_m, b_R4], writes=[bPB[bA], bPB[bB]])
                    lunit(0)
                    for ui, (g, kt) in enumerate(units):
                        k = gpos[(n, g)]
                        v_t, b_v = V3[k % 2]
                        if ui + 1 < len(units):
                            lunit(ui + 1)
                        for par_ in range(2):
                            def half(par):
                                pl = LB[2 * (ui % 2) + par]
                                e_t, b_e = Ex[2 * (ui % 2) + par]
                                S.op("act", lambda a: a.activation(out=e_t[:], in_=PB[pl][:, :], func=AF.Exp, scale=0.125),
                                     reads=[bPB[pl]], writes=[b_e])

                                def mmo(t):
                                    last = None
                                    for j in range(4):
                                        h = 2 * j + par
                                        po = 6 + h // 4
                                        hc = (h % 4) * 65
                                        first = (g == 0 and kt == 0 and h % 4 == 0)
                                        lastf = (g == ng - 1 and kt == 3 and h % 4 == 3)
                                        last = t.matmul(PB[po][:, hc:hc + 65], lhsT=e_t[:, j * 128:(j + 1) * 128],
                                                        rhs=v_t[:, kt, h * 65:(h + 1) * 65], start=first, stop=lastf, skip_group_check=True)
                                    return last
                                S.op("pe", mmo, reads=[b_e, b_v], writes=[bPB[6], bPB[7]])
                            half(par_)
                        if kt == 3:
                            kv_dma(k + 2)

                def final_a(n):
                    for b in range(2):
                        pv = PB[6 + b][:, 0:260].rearrange("p (h d) -> p h d", h=4)
                        S.op("dve", lambda v: v.reciprocal(rden[:, b * 4:(b + 1) * 4].unsqueeze(2), pv[:, :, 64:65]),
                             reads=[bPB[6 + b]], writes=[b_rden])
                        S.op("dve", lambda v: v.tensor_tensor(out=attn[:, b * 256:(b + 1) * 256].rearrange("p (h d) -> p h d", h=4),
                                                              in0=pv[:, :, 0:64],
                                                              in1=rden[:, b * 4:(b + 1) * 4].unsqueeze(2).to_broadcast([128, 4, 64]),
                                                              op=OP.mult), reads=[bPB[6 + b], b_rden], writes=[b_attn])
                    if n == 1:
                        dump("attn1", attn[:], b_attn)

                def final_b1(n):
                    S.op("act", lambda a: a.activation(out=ajunk[:], in_=attn[:], func=AF.Square, accum_out=ast[:, 0:1]),
                         reads=[b_attn], writes=[b_ajunk, b_ast])
                    rstd_from_ssq(ast[:, 0:1], b_ast, 128, 1.0 / 512)
                    S.op("act", lambda a: a.activation(out=ab[:], in_=attn[:], func=AF.Copy, scale=ast[:, 0:1]),
                         reads=[b_attn, b_ast], writes=[b_ab])

                def final_b2(n):
                    def tra(t):
                        last = None
                        for pr in range(4):
                            last = t.transpose(PBh[3][:, pr * 128:(pr + 1) * 128], ab[:, pr * 128:(pr + 1) * 128], ident_b[:])
                        return last
                    S.op("pe", tra, reads=[b_ab, b_identb], writes=[bPB[3]])
                    S.op("act", lambda a: a.activation(out=mixa[:], in_=PBh[3][:, 0:512].rearrange("p (k t) -> p k t", k=4), func=AF.Copy),
                         reads=[bPB[3]], writes=[b_mixas])
                    S.dma(mix_scr[n][:, 0:4, :], mixa[:], reads=[b_mixas], writes=[b_mixa[n]])

                kv_dma(0)
                kv_dma(1)
                prep(0)
                idx(0)
                for n in range(NSLOT):
                    if n + 1 < NSLOT:
                        prep(n + 1)
                    bisect(n)
                    if n >= 1:
                        final_a(n - 1)
                    if n + 1 < NSLOT:
                        idx(n + 1)
                    if n >= 1:
                        final_b1(n - 1)
                    att_main(n)
                    if n >= 1:
                        final_b2(n - 1)
                final_a(NSLOT - 1)
                final_b1(NSLOT - 1)
                final_b2(NSLOT - 1)
                S.barrier()
        S.barrier()

        with ExitStack() as eb:
            Wo, b_Wo = sbt(eb, "Wo", [128, 8, D], BF16)
            W1, b_W1 = sbt(eb, "W1", [128, 8, 4 * D], BF16)
            W2, b_W2 = sbt(eb, "W2", [128, 32, D], BF16)
            b1c, b_b1c = sbt(eb, "b1c", [128, 32], F32)
            S.op("pool", lambda g: g.memset(b1c[:], 0.0), writes=[b_b1c])
            with ExitStack() as ew:
                NSTG = 4
                stg = [sbt(ew, "stg%d" % i, [128, 2 * D], F32) for i in range(NSTG)]
                si = 0
                for kq in range(4):
                    st, b_st = stg[si % NSTG]
                    si += 1
                    S.dma(st[:].rearrange("p (k n) -> p k n", k=2), w_out[kq * 256:(kq + 1) * 256, :].rearrange("(k p) n -> p k n", p=128),
                          writes=[b_st])
                    for k2 in range(2):
                        kc = kq * 2 + k2
                        S.op("dve", lambda v: v.scalar_tensor_tensor(out=Wo[:, kc, :], in0=st[:, k2 * D:(k2 + 1) * D], scalar=gcat[:, kc:kc + 1],
                                                                     in1=gt_row[:, 0, :], op0=OP.mult, op1=OP.mult),
                             reads=[b_st, b_gcat, b_gtrow], writes=[b_Wo])
                def w1_chunk(kc, hf):
                    nonlocal_si[0] += 1
                    st, b_st = stg[nonlocal_si[0] % NSTG]
                    S.dma(st[:], w_ff1[kc * 128:(kc + 1) * 128, hf * 2048:(hf + 1) * 2048], writes=[b_st])

                    def mmb(t):
                        last = None
                        for fc in range(16):
                            last = t.matmul(PB[7][:, fc:fc + 1], lhsT=st[:, fc * 128:(fc + 1) * 128], rhs=modcol[:, 2, kc:kc + 1],
                                            start=True, stop=True)
                        return last
                    S.op("pe", mmb, reads=[b_st, b_modcol], writes=[bPB[7]])
                    S.op("dve", lambda v: v.tensor_tensor(out=b1c[:, hf * 16:(hf + 1) * 16], in0=PB[7][:, 0:16],
                                                          in1=b1c[:, hf * 16:(hf + 1) * 16], op=OP.add),
                         reads=[bPB[7], b_b1c], writes=[b_b1c])
                    S.op("act", lambda a: a.activation(out=W1[:, kc, hf * 2048:(hf + 1) * 2048], in_=st[:], func=AF.Copy,
                                                       scale=gmod[:, 1, kc:kc + 1]),
                         reads=[b_st, b_gmod], writes=[b_W1])

                def w2_chunk(fq):
                    nonlocal_si[0] += 1
                    st, b_st = stg[nonlocal_si[0] % NSTG]
                    S.dma(st[:].rearrange("p (f n) -> p f n", f=2), w_ff2[fq * 256:(fq + 1) * 256, :].rearrange("(f p) n -> p f n", p=128),
                          writes=[b_st])
                    for f_ in range(2):
                        def sc(f):
                            eng = "pool" if f % 2 == 0 else "dve"
                            S.op(eng, lambda g: g.tensor_tensor(out=W2[:, fq * 2 + f, :], in0=st[:, f * D:(f + 1) * D], in1=gt_row[:, 1, :],
                                                                op=OP.mult), reads=[b_st, b_gtrow], writes=[b_W2])
                        sc(f_)
                nonlocal_si = [si]
                for c_ in range(16):
                    w1_chunk(c_ // 2, c_ % 2)
                    w2_chunk(c_)
                dump("b1c", b1c[:], b_b1c)
                S.barrier()
            mixT = [sbt(eb, "mixT%d" % i, [128, 8, 128], BF16) for i in range(1)]
            xt4 = [sbt(eb, "xt4_%d" % i, [128, D], F32) for i in range(1)]
            x1 = [sbt(eb, "x1_%d" % i, [128, D], F32) for i in range(3)]
            xh4, b_xh4 = sbt(eb, "xh4", [128, D], BF16)
            ssq4, b_ssq4 = sbt(eb, "ssq4", [128, 1], F32)
            x1T = [sbt(eb, "x1T%d" % i, [128, 8, 128], BF16) for i in range(2)]
            rr = [sbt(eb, "rr%d" % i, [128, 512], F32) for i in range(2)]
            aTs = [sbt(eb, "aT%d" % i, [128, 32, 128], BF16) for i in range(2)]
            yo = [sbt(eb, "yo%d" % i, [128, D], F32) for i in range(1)]

            def p4_slot(n):
                m_t, b_m = mixT[0]
                x_t, b_x = xt4[0]
                x1_t, b_x1 = x1[n % 3]
                x1T_t, b_x1T = x1T[n % 2]
                yo_t, b_yo = yo[0]
                aT, b_aT = aTs[n % 2]
                S.dma(m_t[:], mix_scr[n], reads=[b_mixa[n], b_mixc[n]], writes=[b_m])
                S.dma(x_t[:], xo[n * 128:(n + 1) * 128, :], writes=[b_x])
                for cg_ in range(2):
                    def oproj(cg):
                        def mm1(t):
                            last = None
                            for kc in range(8):
                                last = t.matmul(PB[cg][:, :], lhsT=m_t[:, kc, :], rhs=Wo[:, kc, cg * 512:(cg + 1) * 512],
                                                start=(kc == 0), stop=(kc == 7))
                            return last
                        S.op("pe", mm1, reads=[b_m, b_Wo], writes=[bPB[cg]])
                        S.op("dve", lambda v: v.tensor_tensor(out=x1_t[:, cg * 512:(cg + 1) * 512], in0=PB[cg][:, :],
                                                              in1=x_t[:, cg * 512:(cg + 1) * 512], op=OP.add),
                             reads=[bPB[cg], b_x], writes=[b_x1])
                    oproj(cg_)
                S.op("act", lambda a: a.activation(out=xh4[:], in_=x1_t[:], func=AF.Square, accum_out=ssq4[:, 0:1]),
                     reads=[b_x1], writes=[b_xh4, b_ssq4])
                rstd_from_ssq(ssq4[:, 0:1], b_ssq4, 128, 1.0 / D)
                S.op("act", lambda a: a.activation(out=xh4[:], in_=x1_t[:], func=AF.Copy, scale=ssq4[:, 0:1]),
                     reads=[b_x1, b_ssq4], writes=[b_xh4])

                def tr4(t):
                    last = None
                    for kc in range(8):
                        last = t.transpose(PBh[2][:, kc * 128:(kc + 1) * 128], xh4[:, kc * 128:(kc + 1) * 128], ident_b[:])
                    return last
                S.op("pe", tr4, reads=[b_xh4, b_identb], writes=[bPB[2]])
                S.op("dve", lambda v: v.tensor_copy(x1T_t[:], PBh[2][:, :].rearrange("p (k t) -> p k t", k=8)),
                     reads=[bPB[2]], writes=[b_x1T])
                S.mark()
                for fq_ in range(8):
                    def ff1(fq):
                        ph = 3 + (fq % 2)
                        r_t, b_r = rr[fq % 2]

                        def mmh(t):
                            last = None
                            for f in range(4):
                                fc = fq * 4 + f
                                for kc in range(8):
                                    last = t.matmul(PB[ph][:, f * 128:(f + 1) * 128], lhsT=W1[:, kc, fc * 128:(fc + 1) * 128],
                                                    rhs=x1T_t[:, kc, :], start=(kc == 0), stop=(kc == 7))
                            return last
                        S.op("pe", mmh, reads=[b_x1T, b_W1], writes=[bPB[ph]])

                        def relu4(a):
                            last = None
                            for f in range(4):
                                fc = fq * 4 + f
                                last = a.activation(out=r_t[:, f * 128:(f + 1) * 128], in_=PB[ph][:, f * 128:(f + 1) * 128], func=AF.Relu,
                                                    bias=b1c[:, fc:fc + 1])
                            return last
                        S.op("act", relu4, reads=[bPB[ph], b_b1c], writes=[b_r])
                        S.op("pool", lambda v: v.tensor_tensor(out=aT[:, fq * 4:(fq + 1) * 4, :], in0=r_t[:].rearrange("p (f t) -> p f t", f=4),
                                                               in1=r_t[:].rearrange("p (f t) -> p f t", f=4), op=OP.mult),
                             reads=[b_r], writes=[b_aT])
                    ff1(fq_)
                S.mark()
                for cg_ in range(2):
                    def ff2(cg):
                        def mm2(t):
                            last = None
                            for fc in range(32):
                                last = t.matmul(PB[5 + cg][:, :], lhsT=aT[:, fc, :], rhs=W2[:, fc, cg * 512:(cg + 1) * 512],
                                                start=(fc == 0), stop=(fc == 31))
                            return last
                        S.op("pe", mm2, reads=[b_aT, b_W2], writes=[bPB[5 + cg]])
                        S.op("dve", lambda v: v.tensor_tensor(out=yo_t[:, cg * 512:(cg + 1) * 512], in0=PB[5 + cg][:, :],
                                                              in1=x1_t[:, cg * 512:(cg + 1) * 512], op=OP.add),
                             reads=[bPB[5 + cg], b_x1], writes=[b_yo])
                    ff2(cg_)
                S.dma(y[n * 128:(n + 1) * 128, :], yo_t[:], reads=[b_yo], writes=[b_y[n]])
                if n == 0:
                    dump("x1_0", x1_t[:], b_x1)

            S.pipeline([S.record(lambda: p4_slot(n)) for n in range(NSLOT)])
            S.barrier()
    return nc


def _rope_tables():
    pos = np.arange(S_LEN, dtype=np.float32)
    inv = (np.float32(500000.0) ** (-np.arange(0, 16, 2, dtype=np.float32) / np.float32(16))).astype(np.float32)
    ang = (pos[:, None] * inv[None, :]).astype(np.float32)
    return np.concatenate([np.cos(ang), np.sin(ang)], axis=1).astype(np.float32)


def _col(v, nchunk):
    return np.ascontiguousarray(np.asarray(v, np.float32).reshape(nchunk, 128).T)


def make_in_maps(x, c, w_ada, b_ada, g_norm1, w_in, g_q, g_k, w_dw, b_dw, g_conv_ln, b_conv_ln,
                 g_out_attn, g_out_conv, w_out, g_norm2, w_ff1, w_ff2):
    f = lambda a: np.ascontiguousarray(np.asarray(a, np.float32))
    x = f(x)
    cs = _rope_tables()
    shared = {
        "w_ada": f(w_ada[0]), "bada_col": _col(b_ada[0], 48), "bada_row": f(b_ada[0]),
        "g1col": _col(g_norm1[0], 8), "g2col": _col(g_norm2[0], 8), "w_in": f(w_in[0]),
        "gq": f(g_q[0]), "gk": f(g_k[0]),
        "wdwT": np.ascontiguousarray(f(w_dw[0]).T.reshape(4, 128, 31).transpose(1, 0, 2)),
        "bdw_col": _col(b_dw[0], 4), "gln": f(g_conv_ln[0]), "bln": f(b_conv_ln[0]),
        "gcat_col": _col(np.concatenate([f(g_out_attn[0]), f(g_out_conv[0])]), 8),
        "w_out": f(w_out[0]), "w_ff1": f(w_ff1[0]), "w_ff2": f(w_ff2[0]),
        "cs_all": np.ascontiguousarray(cs.reshape(NT, 128, 16).transpose(1, 0, 2)),
    }
    in_maps = []
    for core in range(8):
        b, j = core // 4, core % 4
        tiles = [slot_tile(j, n) for n in range(NSLOT)]
        xo = np.concatenate([x[b, t * 128:(t + 1) * 128] for t in tiles], axis=0)
        xhal = np.zeros((NSLOT * 32, D), np.float32)
        hfl = np.zeros((128, NSLOT), np.float32)
        lim = np.zeros((128, NSLOT), np.float32)
        ovr = np.full((128, NSLOT), 1.0e30, np.float32)
        cso = np.zeros((128, NSLOT, 16), np.float32)
        for n, t in enumerate(tiles):
            if t > 0:
                xhal[n * 32:(n + 1) * 32] = x[b, t * 128 - 32:t * 128]
                hfl[:, n] = 1.0
            else:
                xhal[n * 32:(n + 1) * 32] = 1.0
            tq = t * 128 + np.arange(128)
            lim[:, n] = (tq // 64 + 1) * 64 - (slot_ngroups(n) - 1) * 512
            ovr[(tq // 64 + 1) * 64 <= 256, n] = -1.0e29
            cso[:, n, :] = cs[t * 128:(t + 1) * 128]
        m = dict(shared)
        m.update({"xb": x[b], "xo": np.ascontiguousarray(xo), "xhal": xhal, "ccol": _col(np.asarray(c, np.float32)[b], 8),
                  "cs_own": cso, "limrel": lim, "hflag": hfl, "tauovr": ovr})
        in_maps.append(m)
    return in_maps


def assemble(results):
    out = np.zeros((2, S_LEN, D), np.float32)
    for core in range(8):
        b, j = core // 4, core % 4
        yc = np.asarray(results[core]["y"], np.float32)
        for n in range(NSLOT):
            t = slot_tile(j, n)
            out[b, t * 128:(t + 1) * 128] = yc[n * 128:(n + 1) * 128]
    return out


def kernel(**inputs):
    nc = build_nc()
    in_maps = make_in_maps(**inputs)
    res = run_bass_kernel_spmd(nc, in_maps, core_ids=list(range(8)))
    return assemble(res.results)
```
